# Optimizing a Trainium2 kernel written in Bass

```python
import jax, jax.numpy as jnp
from jax import lax
import numpy as np

D_MODEL = 2048
BATCH = 2
SEQ = 4096
DEPTH = 2

HEAD_DIM = 128
N_MLA_HEADS = 4
MLA_Q_LORA = 512
MLA_KV_LORA = 256
MLA_NOPE_DIM = 128
MLA_ROPE_DIM = 64
MLA_V_DIM = 128
N_MOBA_HEADS = 4
MOBA_BLOCK = 256
MOBA_TOPK = 3
MOBA_Q_CHUNK = 64
N_SWA_HEADS = 8
N_SWA_KV_HEADS = 2
SWA_WINDOW = 128
ATTN_Q_BLOCK = 128
D_FF = 5632
ROPE_THETA = 10000.0
NORM_EPS = 1e-6

MIX_WIDTH = N_MLA_HEADS * MLA_V_DIM + N_MOBA_HEADS * HEAD_DIM + N_SWA_HEADS * HEAD_DIM
IN_SIZES = (
    MLA_Q_LORA,
    MLA_KV_LORA,
    MLA_ROPE_DIM,
    N_MOBA_HEADS * HEAD_DIM,
    N_MOBA_HEADS * HEAD_DIM,
    N_MOBA_HEADS * HEAD_DIM,
    N_SWA_HEADS * HEAD_DIM,
    N_SWA_KV_HEADS * HEAD_DIM,
    N_SWA_KV_HEADS * HEAD_DIM,
)
IN_WIDTH = sum(IN_SIZES)

kernel_name = 'hymba_style_mla_moba_swa_macaron'


def rmsnorm(x, g):
    xf = x.astype(jnp.float32)
    y = xf * lax.rsqrt(jnp.mean(xf * xf, axis=-1, keepdims=True) + NORM_EPS)
    return (y * g.astype(jnp.float32)).astype(x.dtype)


def apply_rope(x, theta):
    S, d = x.shape[1], x.shape[-1]
    half = d // 2
    inv_freq = 1.0 / (theta ** (jnp.arange(half, dtype=jnp.float32) * (2.0 / d)))
    ang = jnp.arange(S, dtype=jnp.float32)[:, None] * inv_freq[None, :]
    cos = jnp.cos(ang)[None, :, None, :]
    sin = jnp.sin(ang)[None, :, None, :]
    xf = x.astype(jnp.float32)
    x1, x2 = xf[..., :half], xf[..., half:]
    return jnp.concatenate([x1 * cos - x2 * sin, x2 * cos + x1 * sin], axis=-1).astype(x.dtype)


def swiglu(x, w_gate, w_up, w_down):
    return (jax.nn.silu(x @ w_gate) * (x @ w_up)) @ w_down


def split_columns(z):
    cuts = np.cumsum(np.array(IN_SIZES))[:-1].tolist()
    return jnp.split(z, cuts, axis=-1)


def causal_attention_blocked(q, k, v, scale):
    B, S, H, dq = q.shape
    nb = S // ATTN_Q_BLOCK
    qb = q.reshape(B, nb, ATTN_Q_BLOCK, H, dq).transpose(1, 0, 2, 3, 4)
    kpos = jnp.arange(S)

    def one_block(args):
        qblk, i = args
        qpos = i * ATTN_Q_BLOCK + jnp.arange(ATTN_Q_BLOCK)
        s = jnp.einsum('bqhd,bkhd->bhqk', qblk, k).astype(jnp.float32) * scale
        s = jnp.where(kpos[None, :] <= qpos[:, None], s, -jnp.inf)
        p = jax.nn.softmax(s, axis=-1).astype(v.dtype)
        return jnp.einsum('bhqk,bkhd->bqhd', p, v)

    out = lax.map(one_block, (qb, jnp.arange(nb)))
    return out.transpose(1, 0, 2, 3, 4).reshape(B, S, H, v.shape[-1])


def mla_mixer(c_q, c_kv, k_rope, q_norm, w_uq, kv_norm, w_ukv):
    B, S, _ = c_q.shape
    H = N_MLA_HEADS
    q = (rmsnorm(c_q, q_norm) @ w_uq).reshape(B, S, H, MLA_NOPE_DIM + MLA_ROPE_DIM)
    q_nope, q_pe = q[..., :MLA_NOPE_DIM], q[..., MLA_NOPE_DIM:]
    q_pe = apply_rope(q_pe, ROPE_THETA)
    kv = (rmsnorm(c_kv, kv_norm) @ w_ukv).reshape(B, S, H, MLA_NOPE_DIM + MLA_V_DIM)
    k_nope, v = kv[..., :MLA_NOPE_DIM], kv[..., MLA_NOPE_DIM:]
    k_pe = apply_rope(k_rope[:, :, None, :], ROPE_THETA)
    k = jnp.concatenate([k_nope, jnp.broadcast_to(k_pe, (B, S, H, MLA_ROPE_DIM))], axis=-1)
    q = jnp.concatenate([q_nope, q_pe], axis=-1)
    o = causal_attention_blocked(q, k, v, (MLA_NOPE_DIM + MLA_ROPE_DIM) ** -0.5)
    return o.reshape(B, S, H * MLA_V_DIM)


def moba_attention(q, k, v):
    B, S, H, dh = q.shape
    Sp = -(-S // MOBA_BLOCK) * MOBA_BLOCK
    pad = Sp - S
    if pad:
        widths = ((0, 0), (0, pad), (0, 0), (0, 0))
        q, k, v = jnp.pad(q, widths), jnp.pad(k, widths), jnp.pad(v, widths)
    nkb = Sp // MOBA_BLOCK
    nqc = Sp // MOBA_Q_CHUNK
    top = min(MOBA_TOPK, nkb - 1)
    scale = dh ** -0.5
    kb = k.transpose(0, 2, 1, 3).reshape(B, H, nkb, MOBA_BLOCK, dh)
    vb = v.transpose(0, 2, 1, 3).reshape(B, H, nkb, MOBA_BLOCK, dh)
    kbar = jnp.mean(kb.astype(jnp.float32), axis=3).astype(k.dtype)
    qc = q.transpose(0, 2, 1, 3).reshape(B, H, nqc, MOBA_Q_CHUNK, dh).transpose(2, 0, 1, 3, 4)
    b_idx = jnp.arange(B)[:, None, None, None]
    h_idx = jnp.arange(H)[None, :, None, None]
    blk_ids = jnp.arange(nkb)

    def one_chunk(args):
        qblk, c = args
        qpos = c * MOBA_Q_CHUNK + jnp.arange(MOBA_Q_CHUNK)
        own = qpos[0] // MOBA_BLOCK
        k_own = lax.dynamic_index_in_dim(kb, own, axis=2, keepdims=False)
        v_own = lax.dynamic_index_in_dim(vb, own, axis=2, keepdims=False)
        s_own = jnp.einsum('bhqd,bhkd->bhqk', qblk, k_own).astype(jnp.float32) * scale
        kpos_own = own * MOBA_BLOCK + jnp.arange(MOBA_BLOCK)
        s_own = jnp.where(kpos_own[None, :] <= qpos[:, None], s_own, -jnp.inf)
        if top == 0:
            p = jax.nn.softmax(s_own, axis=-1).astype(v.dtype)
            return jnp.einsum('bhqk,bhkd->bhqd', p, v_own)
        gate = jnp.einsum('bhqd,bhnd->bhqn', qblk, kbar).astype(jnp.float32)
        gate = jnp.where(blk_ids < own, gate, -jnp.inf)
        _, sel = lax.top_k(gate, top)
        sel_valid = sel < own
        k_sel = kb[b_idx, h_idx, sel]
        v_sel = vb[b_idx, h_idx, sel]
        s_sel = jnp.einsum('bhqd,bhqjkd->bhqjk', qblk, k_sel).astype(jnp.float32) * scale
        s_sel = jnp.where(sel_valid[..., None], s_sel, -jnp.inf)
        s_sel = s_sel.reshape(B, H, MOBA_Q_CHUNK, top * MOBA_BLOCK)
        p = jax.nn.softmax(jnp.concatenate([s_sel, s_own], axis=-1), axis=-1).astype(v.dtype)
        p_sel = p[..., :top * MOBA_BLOCK].reshape(B, H, MOBA_Q_CHUNK, top, MOBA_BLOCK)
        p_own = p[..., top * MOBA_BLOCK:]
        return (jnp.einsum('bhqjk,bhqjkd->bhqd', p_sel, v_sel)
                + jnp.einsum('bhqk,bhkd->bhqd', p_own, v_own))

    out = lax.map(one_chunk, (qc, jnp.arange(nqc)))
    out = out.transpose(1, 0, 3, 2, 4).reshape(B, Sp, H * dh)
    return out[:, :S]


def swa_attention(q, k, v, sinks):
    B, S, Hq, dh = q.shape
    Hkv = k.shape[2]
    G = Hq // Hkv
    W = SWA_WINDOW
    nb = S // W
    scale = dh ** -0.5
    qb = q.reshape(B, nb, W, Hkv, G, dh)
    kb = k.reshape(B, nb, W, Hkv, dh)
    vb = v.reshape(B, nb, W, Hkv, dh)
    blk_pad = ((0, 0), (1, 0), (0, 0), (0, 0), (0, 0))
    kk = jnp.concatenate([jnp.pad(kb, blk_pad)[:, :-1], kb], axis=2)
    vv = jnp.concatenate([jnp.pad(vb, blk_pad)[:, :-1], vb], axis=2)
    s = jnp.einsum('bnqhgd,bnkhd->bhgnqk', qb, kk).astype(jnp.float32) * scale
    qpos = jnp.arange(nb)[:, None] * W + jnp.arange(W)[None, :]
    kpos = jnp.arange(nb)[:, None] * W - W + jnp.arange(2 * W)[None, :]
    rel = qpos[:, :, None] - kpos[:, None, :]
    mask = (rel >= 0) & (rel < W) & (kpos[:, None, :] >= 0)
    s = jnp.where(mask, s, -jnp.inf)
    sink = jnp.broadcast_to(sinks.astype(jnp.float32).reshape(1, Hkv, G, 1, 1, 1), s.shape[:-1] + (1,))
    p = jax.nn.softmax(jnp.concatenate([s, sink], axis=-1), axis=-1)[..., :-1].astype(v.dtype)
    o = jnp.einsum('bhgnqk,bnkhd->bnqhgd', p, vv)
    return o.reshape(B, S, Hq * dh)


def setup_inputs(seed: int = 0) -> dict:
    key = jax.random.key(seed)
    ks = jax.random.split(key, 24)
    L = DEPTH

    def normal(k, shape, fan_in):
        return jax.random.normal(k, shape, jnp.float32) * (fan_in ** -0.5)

    def gain(k, shape):
        return 1.0 + 0.01 * jax.random.normal(k, shape, jnp.float32)

    return {
        'x': jax.random.normal(ks[0], (BATCH, SEQ, D_MODEL), jnp.float32),
        'ffn1_norm': gain(ks[1], (L, D_MODEL)),
        'ffn1_w_gate': normal(ks[2], (L, D_MODEL, D_FF), D_MODEL),
        'ffn1_w_up': normal(ks[3], (L, D_MODEL, D_FF), D_MODEL),
        'ffn1_w_down': normal(ks[4], (L, D_FF, D_MODEL), D_FF),
        'attn_norm': gain(ks[5], (L, D_MODEL)),
        'w_in': normal(ks[6], (L, D_MODEL, IN_WIDTH), D_MODEL),
        'mla_q_norm': gain(ks[7], (L, MLA_Q_LORA)),
        'mla_w_uq': normal(ks[8], (L, MLA_Q_LORA, N_MLA_HEADS * (MLA_NOPE_DIM + MLA_ROPE_DIM)), MLA_Q_LORA),
        'mla_kv_norm': gain(ks[9], (L, MLA_KV_LORA)),
        'mla_w_ukv': normal(ks[10], (L, MLA_KV_LORA, N_MLA_HEADS * (MLA_NOPE_DIM + MLA_V_DIM)), MLA_KV_LORA),
        'swa_sinks': 0.5 * jax.random.normal(ks[11], (L, N_SWA_HEADS), jnp.float32),
        'w_out': normal(ks[12], (L, MIX_WIDTH, D_MODEL), MIX_WIDTH),
        'ffn2_norm': gain(ks[13], (L, D_MODEL)),
        'ffn2_w_gate': normal(ks[14], (L, D_MODEL, D_FF), D_MODEL),
        'ffn2_w_up': normal(ks[15], (L, D_MODEL, D_FF), D_MODEL),
        'ffn2_w_down': normal(ks[16], (L, D_FF, D_MODEL), D_FF),
        'final_norm': gain(ks[17], (D_MODEL,)),
    }


def reference(x, ffn1_norm, ffn1_w_gate, ffn1_w_up, ffn1_w_down, attn_norm, w_in,
              mla_q_norm, mla_w_uq, mla_kv_norm, mla_w_ukv, swa_sinks, w_out,
              ffn2_norm, ffn2_w_gate, ffn2_w_up, ffn2_w_down, final_norm):
    B, S, _ = x.shape
    for l in range(DEPTH):
        x = x + 0.5 * swiglu(rmsnorm(x, ffn1_norm[l]), ffn1_w_gate[l], ffn1_w_up[l], ffn1_w_down[l])

        h = rmsnorm(x, attn_norm[l])
        z = h @ w_in[l]
        c_q, c_kv, k_rope, mq, mk, mv, sq, sk, sv = split_columns(z)

        o_mla = mla_mixer(c_q, c_kv, k_rope, mla_q_norm[l], mla_w_uq[l], mla_kv_norm[l], mla_w_ukv[l])

        mq = apply_rope(mq.reshape(B, S, N_MOBA_HEADS, HEAD_DIM), ROPE_THETA)
        mk = apply_rope(mk.reshape(B, S, N_MOBA_HEADS, HEAD_DIM), ROPE_THETA)
        mv = mv.reshape(B, S, N_MOBA_HEADS, HEAD_DIM)
        o_moba = moba_attention(mq, mk, mv)

        sq = apply_rope(sq.reshape(B, S, N_SWA_HEADS, HEAD_DIM), ROPE_THETA)
        sk = apply_rope(sk.reshape(B, S, N_SWA_KV_HEADS, HEAD_DIM), ROPE_THETA)
        sv = sv.reshape(B, S, N_SWA_KV_HEADS, HEAD_DIM)
        o_swa = swa_attention(sq, sk, sv, swa_sinks[l])

        o = jnp.concatenate([o_mla, o_moba, o_swa], axis=-1)
        x = x + o @ w_out[l]

        x = x + 0.5 * swiglu(rmsnorm(x, ffn2_norm[l]), ffn2_w_gate[l], ffn2_w_up[l], ffn2_w_down[l])
    return rmsnorm(x, final_norm)
```

```python
import contextlib
import numpy as np
import ml_dtypes
import concourse.bass as bass
import concourse.mybir as mybir
from concourse.bass_utils import run_bass_kernel_spmd

F32 = mybir.dt.float32
BF16 = mybir.dt.bfloat16
AF = mybir.ActivationFunctionType
ALU = mybir.AluOpType
AX = mybir.AxisListType

D = 2048
DFF = 5632
T = 1024
NCH = 16
INW = 3904
EPS = 1e-6
NEG = -30000.0
GR = 256

KV_MLA_KN = 0
KV_MLA_KPE = 4096
KV_MLA_V = 5120
KV_MOBA_K = 9216
KV_MOBA_V = 13312
KV_SWA_K = 17408
KV_SWA_V = 19456
KVC = 21504
QSC = 16 * 1024 + 2 * 1024

def gcol(l, kind):
    base = l * 54
    return base + {"ffn1": 0, "attn": 16, "ffn2": 32, "qn": 48, "kvn": 52}[kind]
GFINAL = 108
NGC = 124


class V:
    __slots__ = ("ap", "keys")

    def __init__(self, ap, keys):
        self.ap = ap
        self.keys = keys


class TileT:
    def __init__(self, h, space, base, shape, esz):
        self.h, self.space, self.base, self.shape, self.esz = h, space, base, list(shape), esz
        st = [1] * len(shape)
        for i in range(len(shape) - 2, -1, -1):
            st[i] = st[i + 1] * shape[i + 1]
        self.st = st

    def __getitem__(self, idx):
        if not isinstance(idx, tuple):
            idx = (idx,)
        idx = list(idx) + [slice(None)] * (len(self.shape) - len(idx))
        lo = 0
        hi = 0
        for d in range(1, len(self.shape)):
            i = idx[d]
            if isinstance(i, slice):
                a = 0 if i.start is None else i.start
                b = self.shape[d] if i.stop is None else i.stop
            else:
                a, b = i, i + 1
            lo += a * self.st[d]
            hi += (b - 1) * self.st[d]
        lo = self.base + lo * self.esz
        hi = self.base + (hi + 1) * self.esz
        keys = [(self.space, g) for g in range(lo // GR, (hi - 1) // GR + 1)]
        return V(self.h[tuple(idx)], keys)


def DV(ap, key):
    return V(ap, [("D", key)])


class Sched:
    ENG = ("pe", "act", "dve", "pool", "sp")

    def __init__(self, nc):
        self.nc = nc
        self.ops = []
        self.last_w = {}
        self.rd_eng = {}
        self.rd_dma = {}
        self.dma_cnt = {}

    def add(self, eng, fn, reads=(), writes=(), dma_key=None):
        idx = len(self.ops)
        deps = set()
        for r in reads:
            for k in r.keys:
                w = self.last_w.get(k)
                if w is not None:
                    deps.add(w)
        for wv in writes:
            for k in wv.keys:
                w = self.last_w.get(k)
                if w is not None:
                    deps.add(w)
                d = self.rd_eng.get(k)
                if d:
                    deps.update(d.values())
                d = self.rd_dma.get(k)
                if d:
                    deps.update(d)
        is_dma = dma_key is not None
        for r in reads:
            for k in r.keys:
                if is_dma:
                    self.rd_dma.setdefault(k, []).append(idx)
                else:
                    self.rd_eng.setdefault(k, {})[eng] = idx
        for wv in writes:
            for k in wv.keys:
                self.last_w[k] = idx
                self.rd_eng[k] = {}
                self.rd_dma[k] = []
        deps.discard(idx)
        op = {"eng": eng, "fn": fn, "deps": deps, "dma": dma_key, "inc": False}
        if is_dma:
            c = self.dma_cnt.get(dma_key, 0) + 16
            self.dma_cnt[dma_key] = c
            op["val"] = c
        self.ops.append(op)
        return idx

    def emit(self, block, stack):
        nc = self.nc
        ops = self.ops
        for op in ops:
            for d in op["deps"]:
                dop = ops[d]
                if dop["dma"] is None:
                    if dop["eng"] != op["eng"] or dop["eng"] != "pe":
                        dop["inc"] = True
        cnt = {e: 0 for e in self.ENG}
        for op in ops:
            if op["dma"] is None and op["inc"]:
                cnt[op["eng"]] += 1
                op["val"] = cnt[op["eng"]]
        esem = {e: stack.enter_context(nc.semaphore("s_" + e)) for e in ("pe", "act", "dve", "pool")}
        dsem = {k: stack.enter_context(nc.semaphore("d_%s" % k)) for k in self.dma_cnt}
        streams = {e: [i for i, op in enumerate(ops) if op["eng"] == e] for e in self.ENG}

        def run(eng_name, e):
            waited = {}
            for i in streams[eng_name]:
                op = ops[i]
                need = {}
                for d in op["deps"]:
                    dop = ops[d]
                    if dop["dma"] is not None:
                        s, v = dsem[dop["dma"]], dop["val"]
                    else:
                        if dop["eng"] == eng_name and eng_name == "pe":
                            continue
                        s, v = esem[dop["eng"]], dop["val"]
                    if need.get(s.name, (None, 0))[1] < v:
                        need[s.name] = (s, v)
                for nm, (s, v) in need.items():
                    if waited.get(nm, 0) < v:
                        e.wait_ge(s, v)
                        waited[nm] = v
                ins = op["fn"](e)
                if op["dma"] is not None:
                    ins.then_inc(dsem[op["dma"]], 16)
                elif op["inc"]:
                    ins.then_inc(esem[eng_name], 1)
            if eng_name in ("sp", "pool"):
                for k, c in self.dma_cnt.items():
                    if any(ops[i]["dma"] == k for i in streams[eng_name]):
                        e.wait_ge(dsem[k], c)

        @block.tensor
        def _(e):
            run("pe", e)

        @block.scalar
        def _(e):
            run("act", e)

        @block.vector
        def _(e):
            run("dve", e)

        @block.gpsimd
        def _(e):
            run("pool", e)

        @block.sync
        def _(e):
            run("sp", e)


class Builder:
    def __init__(self, nc, stage):
        self.nc = nc
        self.stage = stage
        self.S = Sched(nc)
        self.nslot = 0
        self.bank_rr = 0
        self.uid = 0

    def sb(self, name, shape, dt, off):
        esz = 4 if dt == F32 else 2
        size = esz
        for s in shape[1:]:
            size *= s
        assert off % 32 == 0 and off + size <= 212800, (name, off, size)
        self.uid += 1
        off += 16384
        h = self.nc.alloc_sbuf_tensor_at("%s_%d" % (name, self.uid), list(shape), dt, offset=off)
        return TileT(h, "S", off, shape, esz)

    def dram(self, name, shape, dt, kind):
        return self.nc.dram_tensor(name, list(shape), dt, kind=kind).ap()

    def mm(self, out, lhsT, rhs, start, stop):
        rd = [lhsT, rhs] + ([] if start else [out])
        self.S.add("pe", lambda e: e.matmul(out.ap, lhsT.ap, rhs.ap, start=start, stop=stop), rd, [out])

    def tr(self, out, in_, ident):
        self.S.add("pe", lambda e: e.transpose(out.ap, in_.ap, ident.ap), [in_, ident], [out])

    def act(self, out, in_, func, bias=None, scale=None, extra_reads=()):
        kw = {}
        if bias is not None:
            kw["bias"] = bias.ap if isinstance(bias, V) else bias
        if scale is not None:
            kw["scale"] = scale
        rd = [in_] + list(extra_reads) + ([bias] if isinstance(bias, V) else [])
        self.S.add("act", lambda e: e.activation(out.ap, in_.ap, func, **kw), rd, [out])

    def dve(self, fn, reads, writes):
        self.S.add("dve", fn, reads, writes)

    def tt(self, out, a, b, op):
        self.dve(lambda e: e.tensor_tensor(out.ap, a.ap, b.ap, op), [a, b], [out])

    def dma(self, q, out, in_, key, **kw):
        self.S.add(q, lambda e: e.dma_start(out=out.ap, in_=in_.ap, **kw), [in_], [out], dma_key=key)

    def bank(self, banks=None):
        banks = banks if banks is not None else list(range(8))
        b = banks[self.bank_rr % len(banks)]
        self.bank_rr += 1
        return self.ps[b]

    def wslot(self, shape):
        i = self.nslot % len(self.slots)
        self.nslot += 1
        off = self.slots[i]
        size = 2
        for s in shape[1:]:
            size *= s
        assert size <= self.slot_size, (shape, size)
        return self.sb("w", shape, BF16, off), "w%d" % i

    def build(self):
        nc = self.nc
        st = self.stage
        first = st in ("A0", "ALL", "DBG1", "DBG2", "SEQ")
        last = st in ("B1F", "ALL", "DBG1", "SEQ")
        IN, OUT, INT = "ExternalInput", "ExternalOutput", "Internal"
        dr = self.dram
        self.consts = dr("consts", [128, 128 * 5], BF16, IN)
        self.identf = dr("identf", [128, 128], F32, IN)
        self.gains = dr("gains", [128, NGC], F32, IN)
        self.sinks = dr("sinks", [128, 16], F32, IN)
        self.seq = st == "SEQ"
        self.j = 0
        NJ = 4 if self.seq else 1
        self.rope = dr("rope", [128, 4 * NJ, T], F32, IN)
        self.masks = dr("masks", [128, 13, 512], BF16, IN)
        self.esel = dr("esel", [16, 2048], BF16, IN)
        self.negt = dr("negt", [128, 8 * NJ], F32, IN)
        self.gneg = dr("gneg", [128, 8 * NJ, 16], F32, IN)
        if first:
            self.x_in = dr("x", [T * NJ, D], F32, IN)
        if last:
            self.y_out = dr("y", [T * NJ, D], F32, OUT)
        nl = 2 if st == "ALL" else 1
        self.w = {}
        layers_ffn1 = {"A0": [0], "B0A1": [1], "B1F": [], "ALL": [0, 1], "DBG1": [0], "DBG2": [0], "SEQ": [0, 1]}[st]
        layers_attn = {"A0": [], "B0A1": [0], "B1F": [1], "ALL": [0, 1], "DBG1": [], "DBG2": [], "SEQ": [0, 1]}[st]
        for l in layers_ffn1:
            if st != "DBG2":
                self.w[("g1", l)] = dr("wg1_%d" % l, [D, DFF], F32, IN)
                self.w[("u1", l)] = dr("wu1_%d" % l, [D, DFF], F32, IN)
                self.w[("d1", l)] = dr("wd1_%d" % l, [DFF, D], F32, IN)
            self.w[("in", l)] = dr("win_%d" % l, [D, INW], F32, IN)
            self.w[("uq", l)] = dr("wuq_%d" % l, [512, 768], F32, IN)
            self.w[("ukv", l)] = dr("wukv_%d" % l, [256, 1024], F32, IN)
        for l in layers_attn:
            self.w[("out", l)] = dr("wout_%d" % l, [D, D], F32, IN)
            self.w[("g2", l)] = dr("wg2_%d" % l, [D, DFF], F32, IN)
            self.w[("u2", l)] = dr("wu2_%d" % l, [D, DFF], F32, IN)
            self.w[("d2", l)] = dr("wd2_%d" % l, [DFF, D], F32, IN)
        if st == "SEQ":
            self.xs_j = [dr("xsj%d" % j, [128, NCH * T], F32, INT) for j in range(4)]
            self.qs_j = [dr("qsj%d" % j, [128, QSC], BF16, INT) for j in range(4)]
            self.kvall = [dr("kvall%d" % l, [512, KVC], BF16, INT) for l in range(2)]
            self.set_chunk(0)
        elif st == "ALL":
            self.xs = [dr("xs%d" % l, [128, NCH * T], F32, INT) for l in range(2)]
            self.kvown = [dr("kvown%d" % l, [128, KVC], BF16, INT) for l in range(2)]
            self.kvall = [dr("kvall%d" % l, [512, KVC], BF16, INT) for l in range(2)]
        else:
            self.xs = {}
            self.kvown = {}
            self.kvall = {}
            self.qs = {}
            for l in layers_ffn1:
                self.xs[l] = dr("xs_o", [128, NCH * T], F32, OUT)
                self.kvown[l] = dr("kvown_o", [128, KVC], BF16, OUT)
                self.qs[l] = dr("qs_o", [128, QSC], BF16, OUT)
            for l in layers_attn:
                self.xs[l] = dr("xs_i", [128, NCH * T], F32, IN)
                self.kvown[l] = dr("kvown_i", [128, KVC], BF16, IN)
                self.kvall[l] = dr("kvall_i", [512, KVC], BF16, IN)
                self.qs[l] = dr("qs_i", [128, QSC], BF16, IN)

        sb = self.sb
        self.xT = sb("xT", [128, NCH, T], F32, 0)
        self.hT = sb("hT", [128, NCH, T], BF16, 65536)
        self.actT = sb("actT", [128, 22, T], BF16, 98304)
        self.qside = sb("qside", [128, 16, T], BF16, 98304)
        self.qpe = sb("qpe", [128, 2, T], BF16, 98304 + 32768)
        self.slot_size = 11264
        self.slots = [143360 + i * 11264 for i in range(5)]
        M0 = 199680
        self.sq = [sb("sq", [128, T], BF16, M0 + i * 2048) for i in range(2)]
        self.rstd = sb("rstd", [128, T], F32, M0 + 4096)
        self.silu = [sb("silu", [128, 512], F32, M0 + 8192 + i * 2048) for i in range(1)]
        C0 = M0 + 10240
        self.cst = sb("cst", [128, 640], BF16, C0)
        self.idf = sb("idf", [128, 128], F32, C0 + 1280)
        self.gn = sb("gn", [128, NGC], F32, C0 + 1792)
        self.snk = sb("snk", [128, 16], F32, C0 + 2304)
        self.ngt = sb("ngt", [128, 8], F32, C0 + 2368)
        self.esk = sb("esk", [128, 16], F32, C0 + 2400)
        self.epst = sb("epst", [128, 8], F32, C0 + 2464)
        self.epsb = self.epst[:, 0:1]
        self.ident = self.cst[:, 0:128]
        self.ones = self.cst[:, 128:256]
        self.rot128 = self.cst[:, 256:384]
        self.rot64 = self.cst[:, 384:512]
        self.ps = []
        for b in range(8):
            h = nc.alloc_psum_tensor("ps%d" % b, [128, 512], F32)
            self.ps.append(TileT(h, "P", b * 2048, [128, 512], 4))

        self.dma("sp", self.cst[:, :], DV(self.consts[:, :], "consts"), "c0")
        self.dma("sp", self.idf[:, :], DV(self.identf[:, :], "identf"), "c1")
        self.dma("sp", self.gn[:, :], DV(self.gains[:, :], "gains"), "c2")
        self.dma("sp", self.snk[:, :], DV(self.sinks[:, :], "sinks"), "c3")
        if not self.seq:
            self.dma("sp", self.ngt[:, :], DV(self.negt[:, :], "negt"), "c4")
        self.act(self.esk[:, :], self.snk[:, :], AF.Exp)
        ep = self.epst[:, :]
        self.dve(lambda e: e.memset(ep.ap, EPS), [], [ep])

        if st == "SEQ":
            for l in range(2):
                for j in range(4):
                    self.set_chunk(j)
                    if l == 0:
                        self.load_x()
                    else:
                        self.move_x(True)
                    self.ffn(l, 1)
                    self.seg_win(l)
                    self.spill_q(l)
                for j in range(4):
                    self.set_chunk(j)
                    self.restore_q(l)
                    self.seg_attn(l)
                    self.ffn(l, 2)
                    if l == 0:
                        self.move_x(False)
                    else:
                        self.final()
        elif st == "ALL":
            self.load_x()
            for l in range(2):
                self.ffn(l, 1)
                self.seg_win(l)
                self.exchange(l)
                self.seg_attn(l)
                self.ffn(l, 2)
            self.final()
        elif st == "DBG1":
            self.load_x()
            self.ffn(0, 1)
            self.final(norm=False)
        elif st == "DBG2":
            self.load_x()
            self.seg_win(0)
            self.spill_q(0)
        elif st == "A0":
            self.load_x()
            self.ffn(0, 1)
            self.seg_win(0)
            self.spill_q(0)
        elif st == "B0A1":
            self.restore_q(0)
            self.seg_attn(0)
            self.ffn(0, 2)
            self.ffn(1, 1)
            self.seg_win(1)
            self.spill_q(1)
        elif st == "B1F":
            self.restore_q(1)
            self.seg_attn(1)
            self.ffn(1, 2)
            self.final()

        with contextlib.ExitStack() as stack:
            block = stack.enter_context(nc.Block())
            self.S.emit(block, stack)
        return nc

    def set_chunk(self, j):
        self.j = j
        self.xs = {0: self.xs_j[j], 1: self.xs_j[j]}
        self.qs = {0: self.qs_j[j], 1: self.qs_j[j]}
        self.kvown = {l: self.kvall[l][j * 128:(j + 1) * 128, :] for l in range(2)}

    def move_x(self, load):
        xs = self.xs[0]
        for c4 in range(4):
            sv = V(self.xT.h[:, c4 * 4:(c4 + 1) * 4, :].rearrange("p c t -> p (c t)"), self.xT[:, c4 * 4:(c4 + 1) * 4, :].keys)
            dv = DV(xs[:, c4 * 4096:(c4 + 1) * 4096], "xs%d_%d" % (self.j, c4))
            if load:
                self.dma("sp", sv, dv, "xsp%d" % c4)
            else:
                self.dma("sp", dv, sv, "xsp%d" % c4)

    def kakeys(self, l, jp):
        if self.seq:
            return [("D", "kv%d_%d" % (l, jp), g) for g in range(KVC // 512)]
        return [("D", "kvall%d" % l)]

    def load_x(self):
        stage = [self.sb("xst", [128, D], F32, 98304 + i * 8192) for i in range(2)]
        for tt in range(8):
            s = stage[tt % 2]
            self.dma("sp", s[:, :], DV(self.x_in[self.j * T + tt * 128:self.j * T + (tt + 1) * 128, :], "x"), "xst%d" % (tt % 2))
            for c4 in range(4):
                pb = self.bank()
                for k in range(4):
                    c = c4 * 4 + k
                    self.tr(pb[:, k * 128:(k + 1) * 128], s[:, c * 128:(c + 1) * 128], self.idf[:, :])
                o = self.xT[:, c4 * 4:c4 * 4 + 4, tt * 128:(tt + 1) * 128]
                src = V(pb.h[:, :].rearrange("p (k t) -> p k t", k=4), pb[:, :].keys)
                if c4 % 2 == 0:
                    self.S.add("act", lambda e, o=o, src=src: e.copy(o.ap, src.ap), [src], [o])
                else:
                    self.dve(lambda e, o=o, src=src: e.tensor_copy(o.ap, src.ap), [src], [o])

    def rmsnorm(self, src, nch, dim, g0, dst, in_place=False):
        ssq = [self.bank(), self.bank()]
        for c in range(nch):
            sq = self.sq[c % 2]
            self.act(sq[:, :], src(c), AF.Square)
            for tg in range(2):
                self.mm(ssq[tg][:, :], self.ones, sq[:, tg * 512:(tg + 1) * 512], c == 0, c == nch - 1)
        for tg in range(2):
            r = self.rstd[:, tg * 512:(tg + 1) * 512]
            self.act(r, ssq[tg][:, :], AF.Sqrt, bias=self.epsb, scale=1.0 / dim)
            self.dve(lambda e, r=r: e.reciprocal(r.ap, r.ap), [r], [r])
        for c in range(nch):
            s, d = src(c), dst(c)
            g = self.gn[:, g0 + c:g0 + c + 1]
            r = self.rstd[:, :]
            self.dve(lambda e, s=s, d=d, g=g, r=r: e.scalar_tensor_tensor(d.ap, s.ap, g.ap, r.ap, ALU.mult, ALU.mult),
                     [s, g, r], [d])

    def xnorm(self, g0):
        self.rmsnorm(lambda c: self.xT[:, c, :], NCH, D, g0, lambda c: self.hT[:, c, :])

    def wload(self, dst, src_ap, key, wkey):
        self.dma("pool", dst, DV(src_ap, wkey), key)

    def ffn(self, l, which):
        self.xnorm(gcol(l, "ffn%d" % which))
        wg, wu, wd = self.w[("g%d" % which, l)], self.w[("u%d" % which, l)], self.w[("d%d" % which, l)]
        FB = list(range(8))
        for half in range(2):
            tasks = []
            for blk in range(11):
                c0 = half * 2816 + blk * 256

                def load(c0=c0):
                    tg_, kg = self.wslot([128, NCH, 256])
                    tu_, ku = self.wslot([128, NCH, 256])
                    self.wload(tg_[:, :, :], wg[:, c0:c0 + 256].rearrange("(c p) n -> p c n", p=128), kg, "wg")
                    self.wload(tu_[:, :, :], wu[:, c0:c0 + 256].rearrange("(c p) n -> p c n", p=128), ku, "wu")
                    return tg_, tu_

                def comp(ws, blk=blk):
                    tg_, tu_ = ws
                    for sub in range(2):
                        f = blk * 2 + sub
                        for tg in range(2):
                            gp, up = self.bank(FB), self.bank(FB)
                            for c in range(NCH):
                                self.mm(gp[:, :], tg_[:, c, sub * 128:(sub + 1) * 128], self.hT[:, c, tg * 512:(tg + 1) * 512], c == 0, c == NCH - 1)
                            for c in range(NCH):
                                self.mm(up[:, :], tu_[:, c, sub * 128:(sub + 1) * 128], self.hT[:, c, tg * 512:(tg + 1) * 512], c == 0, c == NCH - 1)
                            sl = self.silu[0]
                            self.act(sl[:, :], gp[:, :], AF.Silu)
                            self.tt(self.actT[:, f, tg * 512:(tg + 1) * 512], sl[:, :], up[:, :], ALU.mult)
                tasks.append((load, comp))
            for ob in range(8):
                def load(ob=ob):
                    td, kd = self.wslot([128, 22, 256])
                    self.wload(td[:, :, :], wd[half * 2816:(half + 1) * 2816, ob * 256:(ob + 1) * 256].rearrange("(c p) n -> p c n", p=128), kd, "wd")
                    return td

                def comp(td, ob=ob):
                    for sub in range(2):
                        o = ob * 2 + sub
                        for tg in range(2):
                            op_ = self.bank(FB)
                            for fc in range(22):
                                self.mm(op_[:, :], td[:, fc, sub * 128:(sub + 1) * 128], self.actT[:, fc, tg * 512:(tg + 1) * 512], fc == 0, fc == 21)
                            xv = self.xT[:, o, tg * 512:(tg + 1) * 512]
                            self.dve(lambda e, xv=xv, op_=op_: e.scalar_tensor_tensor(xv.ap, op_[:, :].ap, 0.5, xv.ap, ALU.mult, ALU.add),
                                     [op_[:, :], xv], [xv])
                tasks.append((load, comp))
            self.pipeline(tasks)

    def pipeline(self, tasks):
        cur = tasks[0][0]()
        for i in range(len(tasks)):
            nxt = tasks[i + 1][0]() if i + 1 < len(tasks) else None
            tasks[i][1](cur)
            cur = nxt

    def final(self, norm=True):
        if norm:
            self.rmsnorm(lambda c: self.xT[:, c, :], NCH, D, GFINAL, lambda c: self.xT[:, c, :])
        stage = [self.sb("yst", [128, D], F32, 65536 + i * 8192) for i in range(2)]
        for tt in range(8):
            s = stage[tt % 2]
            for c4 in range(4):
                pb = self.bank()
                for k in range(4):
                    c = c4 * 4 + k
                    self.tr(pb[:, k * 128:(k + 1) * 128], self.xT[:, c, tt * 128:(tt + 1) * 128], self.idf[:, :])
                o = s[:, c4 * 512:(c4 + 1) * 512]
                if c4 % 2 == 0:
                    self.S.add("act", lambda e, o=o, pb=pb: e.copy(o.ap, pb[:, :].ap), [pb[:, :]], [o])
                else:
                    self.dve(lambda e, o=o, pb=pb: e.tensor_copy(o.ap, pb[:, :].ap), [pb[:, :]], [o])
            self.dma("sp", DV(self.y_out[self.j * T + tt * 128:self.j * T + (tt + 1) * 128, :], "y%d_%d" % (self.j, tt)), s[:, :], "yst%d" % (tt % 2))

    def kvkeys(self, l, spans):
        keys = []
        for (a, n) in spans:
            for g in range(a // 512, (a + n - 1) // 512 + 1):
                k = ("D", "kv%d_%d" % (l, self.j), g) if self.seq else ("D", "kvown%d" % l, g)
                if k not in keys:
                    keys.append(k)
        return keys

    def seg_win(self, l):
        sb = self.sb
        self.xnorm(gcol(l, "attn"))
        xs = self.xs[l]
        for c4 in range(4):
            src = V(self.xT.h[:, c4 * 4:(c4 + 1) * 4, :].rearrange("p c t -> p (c t)"), self.xT[:, c4 * 4:(c4 + 1) * 4, :].keys)
            self.dma("sp", DV(xs[:, c4 * 4096:(c4 + 1) * 4096], "xs%d_%d" % (self.j if self.seq else l, c4)), src, "xsp%d" % c4)
        ropeT = sb("ropeT", [128, 4, T], F32, 0)
        cq = sb("cq", [128, 4, T], F32, 16384)
        ckv = sb("ckv", [128, 2, T], F32, 32768)
        cqn = sb("cqn", [128, 4, T], BF16, 40960)
        ckvn = sb("ckvn", [128, 2, T], BF16, 49152)
        t1 = [sb("t1", [128, 512], F32, 53248 + i * 2048) for i in range(2)]
        t2 = [sb("t2", [128, 512], F32, 57344 + i * 2048) for i in range(2)]
        zb = [sb("zb", [128, 512], BF16, 61440 + i * 1024) for i in range(2)]
        kst = [sb("kst", [128, 512], BF16, 63488 + i * 1024) for i in range(2)]
        for i in range(4):
            self.dma("sp", ropeT[:, i, :], DV(self.rope[:, self.j * 4 + i, :], "rope"), "rp%d" % i)
        win, wuq, wukv = self.w[("in", l)], self.w[("uq", l)], self.w[("ukv", l)]
        kvo = self.kvown[l]
        cnt = {"r": 0, "k": 0}

        import os
        DBGF = int(os.environ.get("DBGF", "0"))

        def kv_store(stage_v, col, n):
            if DBGF & 1:
                return
            self.dma("sp", V(kvo[:, col:col + n], self.kvkeys(l, [(col, n)])), stage_v, "kst%d" % (cnt["k"] % 2))

        def rope_ep(zp, rp, ti, tg, dst):
            i = cnt["r"] % 2
            cnt["r"] += 1
            self.tt(t1[i][:, :], ropeT[:, ti, tg * 512:(tg + 1) * 512], zp[:, :], ALU.mult)
            self.tt(t2[i][:, :], ropeT[:, ti + 1, tg * 512:(tg + 1) * 512], rp[:, :], ALU.mult)
            self.tt(dst, t1[i][:, :], t2[i][:, :], ALU.add)

        def make_rot(w_, c0, ncols, half, nch):
            w_r, _ = self.wslot([128, nch, 256])
            src = w_.h[:, :, c0:c0 + ncols].rearrange("p c (b t r) -> p c b t r", t=2, r=half)
            dst = w_r.h[:, :, 0:ncols].rearrange("p c (b t r) -> p c b t r", t=2, r=half)
            sk, dk = w_[:, :, c0:c0 + ncols].keys, w_r[:, :, 0:ncols].keys
            a0, b1 = V(dst[:, :, :, 0, :], dk), V(src[:, :, :, 1, :], sk)
            a1, b0 = V(dst[:, :, :, 1, :], dk), V(src[:, :, :, 0, :], sk)
            self.dve(lambda e: e.tensor_scalar(a0.ap, b1.ap, -1.0, None, ALU.mult), [b1], [a0])
            self.dve(lambda e: e.tensor_copy(a1.ap, b0.ap), [b0], [a1])
            return w_r

        def fm_block(col0, ncols, dup=False, half=None):
            def load():
                w_, k_ = self.wslot([128, NCH, 256])
                if dup:
                    self.wload(w_[:, :, 0:128], win[:, col0:col0 + 128].rearrange("(c p) n -> p c n", p=128), k_, "win")
                    a, b = w_[:, :, 64:128], w_[:, :, 0:64]
                    self.dve(lambda e: e.tensor_copy(a.ap, b.ap), [b], [a])
                    nc_ = 128
                else:
                    self.wload(w_[:, :, 0:ncols], win[:, col0:col0 + ncols].rearrange("(c p) n -> p c n", p=128), k_, "win")
                    nc_ = ncols
                if half is not None:
                    return w_, make_rot(w_, 0, nc_, half, NCH)
                return w_, None
            return load

        def fm_comp(ntile, ep):
            def comp(ws):
                w_, w_r = ws
                for s_ in range(ntile):
                    for tg in range(2):
                        zp = self.bank()
                        for c in range(NCH):
                            self.mm(zp[:, :], w_[:, c, s_ * 128:(s_ + 1) * 128], self.hT[:, c, tg * 512:(tg + 1) * 512], c == 0, c == NCH - 1)
                        rp = None
                        if w_r is not None:
                            rp = self.bank()
                            for c in range(NCH):
                                self.mm(rp[:, :], w_r[:, c, s_ * 128:(s_ + 1) * 128], self.hT[:, c, tg * 512:(tg + 1) * 512], c == 0, c == NCH - 1)
                        ep(s_, tg, zp, rp)
            return comp

        def ep_copy(dst_t, c0):
            def ep(s_, tg, zp, rp):
                d = dst_t[:, c0 + s_, tg * 512:(tg + 1) * 512]
                self.S.add("act", lambda e: e.copy(d.ap, zp[:, :].ap), [zp[:, :]], [d])
            return ep

        def ep_rope_q(ch0):
            def ep(s_, tg, zp, rp):
                rope_ep(zp, rp, 0, tg, self.qside[:, ch0 + s_, tg * 512:(tg + 1) * 512])
            return ep

        def ep_rope_k(col_base, ti):
            def ep(s_, tg, zp, rp):
                st_ = kst[cnt["k"] % 2]
                rope_ep(zp, rp, ti, tg, st_[:, :])
                kv_store(st_[:, :], col_base + s_ * 1024 + tg * 512, 512)
                cnt["k"] += 1
            return ep

        def v_comp(col_base, nh):
            def comp(ws):
                w_ = ws[0]
                for tt in range(8):
                    vp = self.bank()
                    for c in range(NCH):
                        self.mm(vp[:, 0:nh * 128], self.hT[:, c, tt * 128:(tt + 1) * 128], w_[:, c, 0:nh * 128], c == 0, c == NCH - 1)
                    st_ = kst[cnt["k"] % 2]
                    sv = st_[:, 0:nh * 128]
                    self.S.add("act", lambda e, sv=sv, vp=vp: e.copy(sv.ap, vp[:, 0:nh * 128].ap), [vp[:, 0:nh * 128]], [sv])
                    dst = kvo[:, col_base:col_base + nh * 1024].rearrange("p (h r) -> p h r", h=nh)[:, :, tt * 128:(tt + 1) * 128]
                    keys = self.kvkeys(l, [(col_base + h_ * 1024 + tt * 128, 128) for h_ in range(nh)])
                    src = V(st_.h[:, 0:nh * 128].rearrange("p (h r) -> p h r", h=nh), sv.keys)
                    self.dma("sp", V(dst, keys), src, "kst%d" % (cnt["k"] % 2))
                    cnt["k"] += 1
            return comp

        def uq_load():
            w_, k_ = self.wslot([128, 4, 1024])
            self.wload(w_[:, :, 0:768], wuq[:, :].rearrange("(c p) n -> p c n", p=128), k_, "wuq")
            for c in range(4):
                dst = V(w_.h[:, c, 768:1024].rearrange("p (h r) -> p h r", h=4), w_[:, c, 768:1024].keys)
                self.wload(dst, wuq[c * 128:(c + 1) * 128, :].rearrange("p (h r) -> p h r", h=4)[:, :, 128:192], "uqpe%d" % c, "wuq")
            return w_, make_rot(w_, 768, 256, 32, 4)

        def uq_comp(ws):
            w_, w_r = ws
            self.rmsnorm(lambda c: cq[:, c, :], 4, 512, gcol(l, "qn"), lambda c: cqn[:, c, :])
            for h in range(4):
                for tg in range(2):
                    zp = self.bank()
                    for c in range(4):
                        self.mm(zp[:, :], w_[:, c, h * 192:h * 192 + 128], cqn[:, c, tg * 512:(tg + 1) * 512], c == 0, c == 3)
                    d = self.qside[:, h, tg * 512:(tg + 1) * 512]
                    self.S.add("act", lambda e, d=d, zp=zp: e.copy(d.ap, zp[:, :].ap), [zp[:, :]], [d])
            for pr in range(2):
                for tg in range(2):
                    zp, rp = self.bank(), self.bank()
                    for c in range(4):
                        lw = w_[:, c, 768 + pr * 128:768 + (pr + 1) * 128]
                        self.mm(zp[:, :], lw, cqn[:, c, tg * 512:(tg + 1) * 512], c == 0, c == 3)
                    for c in range(4):
                        self.mm(rp[:, :], w_r[:, c, pr * 128:(pr + 1) * 128], cqn[:, c, tg * 512:(tg + 1) * 512], c == 0, c == 3)
                    rope_ep(zp, rp, 2, tg, self.qpe[:, pr, tg * 512:(tg + 1) * 512])

        def ukv_load():
            w_, k_ = self.wslot([128, 2, 1024])
            self.wload(w_[:, :, :], wukv[:, :].rearrange("(c p) n -> p c n", p=128), k_, "wukv")
            return w_

        def ukv_comp(w_):
            self.rmsnorm(lambda c: ckv[:, c, :], 2, 256, gcol(l, "kvn"), lambda c: ckvn[:, c, :])
            for h in range(4):
                for tg in range(2):
                    zp = self.bank()
                    for c in range(2):
                        self.mm(zp[:, :], w_[:, c, h * 256:h * 256 + 128], ckvn[:, c, tg * 512:(tg + 1) * 512], c == 0, c == 1)
                    st_ = kst[cnt["k"] % 2]
                    self.S.add("act", lambda e, st_=st_, zp=zp: e.copy(st_[:, :].ap, zp[:, :].ap), [zp[:, :]], [st_[:, :]])
                    kv_store(st_[:, :], KV_MLA_KN + h * 1024 + tg * 512, 512)
                    cnt["k"] += 1
            for tt in range(8):
                vp = self.bank()
                for c in range(2):
                    rw = V(w_.h[:, c, :].rearrange("p (h r) -> p h r", h=4)[:, :, 128:256], w_[:, c, :].keys)
                    self.mm(vp[:, :], ckvn[:, c, tt * 128:(tt + 1) * 128], rw, c == 0, c == 1)
                st_ = kst[cnt["k"] % 2]
                self.S.add("act", lambda e, st_=st_, vp=vp: e.copy(st_[:, :].ap, vp[:, :].ap), [vp[:, :]], [st_[:, :]])
                dst = kvo[:, KV_MLA_V:KV_MLA_V + 4096].rearrange("p (h r) -> p h r", h=4)[:, :, tt * 128:(tt + 1) * 128]
                keys = self.kvkeys(l, [(KV_MLA_V + h_ * 1024 + tt * 128, 128) for h_ in range(4)])
                src = V(st_.h[:, :].rearrange("p (h r) -> p h r", h=4), st_[:, :].keys)
                self.dma("sp", V(dst, keys), src, "kst%d" % (cnt["k"] % 2))
                cnt["k"] += 1

        tasks = []
        tasks.append((fm_block(0, 256), fm_comp(2, ep_copy(cq, 0))))
        tasks.append((fm_block(256, 256), fm_comp(2, ep_copy(cq, 2))))
        tasks.append((fm_block(512, 256), fm_comp(2, ep_copy(ckv, 0))))
        tasks.append((fm_block(768, 64, dup=True, half=32), fm_comp(1, ep_rope_k(KV_MLA_KPE, 2))))
        tasks.append((uq_load, uq_comp))
        tasks.append((ukv_load, ukv_comp))
        for b in range(2):
            tasks.append((fm_block(832 + 256 * b, 256, half=64), fm_comp(2, ep_rope_q(4 + 2 * b))))
        for b in range(2):
            tasks.append((fm_block(1344 + 256 * b, 256, half=64), fm_comp(2, ep_rope_k(KV_MOBA_K + 2 * b * 1024, 0))))
        for b in range(4):
            tasks.append((fm_block(2368 + 256 * b, 256, half=64), fm_comp(2, ep_rope_q(8 + 2 * b))))
        tasks.append((fm_block(3392, 256, half=64), fm_comp(2, ep_rope_k(KV_SWA_K, 0))))
        for b in range(2):
            tasks.append((fm_block(1856 + 256 * b, 256), v_comp(KV_MOBA_V + 2 * b * 1024, 2)))
        tasks.append((fm_block(3648, 256), v_comp(KV_SWA_V, 2)))
        import os
        nt = int(os.environ.get("WIN_NT", "99"))
        self.pipeline(tasks[:nt])

    def kv_all_keys(self, l):
        return [("D", "kvown%d" % l, g) for g in range(KVC // 512)]

    def spill_q(self, l):
        qs = self.qs[l]
        src = V(self.qside.h[:, :, :].rearrange("p c t -> p (c t)"), self.qside[:, :, :].keys)
        self.dma("sp", DV(qs[:, 0:16384], "qs_a%d" % self.j), src, "qsp0")
        src = V(self.qpe.h[:, :, :].rearrange("p c t -> p (c t)"), self.qpe[:, :, :].keys)
        self.dma("sp", DV(qs[:, 16384:QSC], "qs_b%d" % self.j), src, "qsp1")

    def restore_q(self, l):
        qs = self.qs[l]
        dst = V(self.qside.h[:, :, :].rearrange("p c t -> p (c t)"), self.qside[:, :, :].keys)
        self.dma("sp", dst, DV(qs[:, 0:16384], "qs_a%d" % self.j), "qsp0")
        dst = V(self.qpe.h[:, :, :].rearrange("p c t -> p (c t)"), self.qpe[:, :, :].keys)
        self.dma("sp", dst, DV(qs[:, 16384:QSC], "qs_b%d" % self.j), "qsp1")

    def exchange(self, l):
        kvo, kva = self.kvown[l], self.kvall[l]
        rd = V(kvo[:, :], self.kv_all_keys(l))
        wr = DV(kva[:, :], "kvall%d" % l)
        self.S.add("pool", lambda e: e.collective_compute("AllGather", ALU.bypass, replica_groups=[[0, 1, 2, 3], [4, 5, 6, 7]],
                                                          ins=[kvo[:, :]], outs=[kva[:, :]]),
                   [rd], [wr], dma_key="cc%d" % l)

    def seg_attn(self, l):
        sb = self.sb
        kvo, kva = self.kvown[l], self.kvall[l]
        KA = "kvall%d" % l
        jK = [sb("jK", [128, 5, T], BF16, i * 20480) for i in range(2)]
        jV = [sb("jV", [128, 5, 8, 128], BF16, i * 20480 + 10240) for i in range(2)]
        kpe = sb("kpe", [128, 4, T], BF16, 40960)
        mk = sb("mk", [128, 13, 512], BF16, 49152)
        esel = sb("esel", [16, 2048], BF16, 62464)
        negmT = [sb("negmT", [16, T], BF16, 66560 + i * 2048) for i in range(2)]
        PT = [sb("PT", [128, 512], BF16, 70656 + i * 1024) for i in range(4)]
        rs = [sb("rs", [128, 512], F32, 74752 + i * 2048) for i in range(2)]
        gsc = sb("gsc", [128, 8, 16], F32, 78848)
        top8 = sb("top8", [128, 8, 8], F32, 79360)
        thr = sb("thr", [128, 8], F32, 79616)
        nm = [sb("nm", [128, 16], F32, 79648 + i * 64) for i in range(2)]
        kbar = sb("kbar", [128, 16], F32, 79776)
        kbarb = sb("kbarb", [128, 16], BF16, 79840)
        gnegT = sb("gnegT", [128, 8, 16], F32, 79872)
        SB_ = [4, 5]
        MISC = [6, 7]
        st = {"pt": 0, "rs": 0, "job": 0}

        for m0 in (0, 4, 8):
            n = 5 if m0 == 8 else 4
            self.dma("sp", mk[:, m0:m0 + n, :], DV(self.masks[:, m0:m0 + n, :], "masks"), "mk%d" % m0)
        self.dma("sp", esel[:, :], DV(self.esel[:, :], "esel"), "esel")
        self.dma("sp", V(gnegT.h[:, :, :], gnegT[:, :, :].keys), DV(self.gneg[:, self.j * 8:(self.j + 1) * 8, :], "gneg"), "gneg")
        if self.seq:
            self.dma("sp", self.ngt[:, :], DV(self.negt[:, self.j * 8:(self.j + 1) * 8], "negt"), "c4")
        for jp in range(3):
            self.dma("sp", kpe[:, jp, :], V(kva[jp * 128:(jp + 1) * 128, KV_MLA_KPE:KV_MLA_KPE + 1024], self.kakeys(l, jp)), "kpe%d" % jp)
        self.dma("sp", kpe[:, 3, :], V(kvo[:, KV_MLA_KPE:KV_MLA_KPE + 1024], self.kvkeys(l, [(KV_MLA_KPE, 1024)])), "kpe3")

        def run_job(steps, scale, outv, sink=None):
            par = st["job"] % 2
            st["job"] += 1
            acc, ssum = self.ps[2 * par], self.ps[2 * par + 1]
            nst = len(steps)
            sbanks = [None] * nst

            def qk(i):
                sbanks[i] = self.bank(SB_)
                mms = steps[i]["qk"] + steps[i].get("mask", [])
                for k, (a, b) in enumerate(mms):
                    self.mm(sbanks[i][:, :], a, b, k == 0, k == len(mms) - 1)
            qk(0)
            for i in range(nst):
                if i + 1 < nst:
                    qk(i + 1)
                pt = PT[st["pt"] % 4]
                st["pt"] += 1
                self.act(pt[:, :], sbanks[i][:, :], AF.Exp, bias=steps[i].get("bias"), scale=scale)
                self.mm(acc[:, :], steps[i]["v"], pt[:, :], i == 0, i == nst - 1)
                self.mm(ssum[:, :], self.ones, pt[:, :], i == 0, i == nst - 1)
            r = rs[st["rs"] % 2]
            st["rs"] += 1
            if sink is not None:
                self.dve(lambda e: e.tensor_scalar(r[:, :].ap, ssum[:, :].ap, sink.ap, None, ALU.add), [ssum[:, :], sink], [r[:, :]])
                self.dve(lambda e: e.reciprocal(r[:, :].ap, r[:, :].ap), [r[:, :]], [r[:, :]])
            else:
                self.dve(lambda e: e.reciprocal(r[:, :].ap, ssum[:, :].ap), [ssum[:, :]], [r[:, :]])
            self.tt(outv, acc[:, :], r[:, :], ALU.mult)

        def ident_mask(m):
            return (self.ident, mk[:, m, :])

        tasks = []
        for h in range(4):
            def load(h=h):
                par = h % 2
                K_, V_ = jK[par], jV[par]
                for jp in (range(self.j) if self.seq else range(3)):
                    self.dma("sp", K_[:, jp, :], V(kva[jp * 128:(jp + 1) * 128, KV_MLA_KN + h * 1024:KV_MLA_KN + (h + 1) * 1024], self.kakeys(l, jp)), "jk%d_%d" % (par, jp))
                    self.dma("sp", V(V_.h[:, jp, :, :], V_[:, jp, :, :].keys),
                             V(kva[jp * 128:(jp + 1) * 128, KV_MLA_V + h * 1024:KV_MLA_V + (h + 1) * 1024].rearrange("p (t d) -> p t d", t=8), self.kakeys(l, jp)), "jv%d_%d" % (par, jp))
                self.dma("sp", K_[:, 3, :], V(kvo[:, KV_MLA_KN + h * 1024:KV_MLA_KN + (h + 1) * 1024], self.kvkeys(l, [(KV_MLA_KN + h * 1024, 1024)])), "jk%d_3" % par)
                self.dma("sp", V(V_.h[:, 3, :, :], V_[:, 3, :, :].keys),
                         V(kvo[:, KV_MLA_V + h * 1024:KV_MLA_V + (h + 1) * 1024].rearrange("p (t d) -> p t d", t=8), self.kvkeys(l, [(KV_MLA_V + h * 1024, 1024)])), "jv%d_3" % par)
                return K_, V_

            def comp(kv, h=h):
                K_, V_ = kv
                p0 = (h % 2) * 64
                for qg in range(2):
                    qn = self.qside[:, h, qg * 512:(qg + 1) * 512]
                    qp = self.qpe[p0:p0 + 64, h // 2, qg * 512:(qg + 1) * 512]
                    steps = []
                    for jp in (range(self.j) if self.seq else range(3)):
                        for i in range(8):
                            steps.append({"qk": [(K_[:, jp, i * 128:(i + 1) * 128], qn), (kpe[p0:p0 + 64, jp, i * 128:(i + 1) * 128], qp)],
                                          "v": V_[:, jp, i, :], "bias": self.ngt[:, jp:jp + 1]})
                    for i in range(4 * qg + 4):
                        s_ = {"qk": [(K_[:, 3, i * 128:(i + 1) * 128], qn), (kpe[p0:p0 + 64, 3, i * 128:(i + 1) * 128], qp)], "v": V_[:, 3, i, :]}
                        if i >= 4 * qg:
                            s_["mask"] = [ident_mask(i - 4 * qg)]
                        steps.append(s_)
                    run_job(steps, 192.0 ** -0.5, self.qside[:, h, qg * 512:(qg + 1) * 512])
            tasks.append((load, comp))
        for h in range(4):
            def load(h=h):
                par = h % 2
                K_, V_ = jK[par], jV[par]
                for jp in range(5):
                    if jp < 4:
                        ks = V(kva[jp * 128:(jp + 1) * 128, KV_MOBA_K + h * 1024:KV_MOBA_K + (h + 1) * 1024], self.kakeys(l, jp))
                        vs = V(kva[jp * 128:(jp + 1) * 128, KV_MOBA_V + h * 1024:KV_MOBA_V + (h + 1) * 1024].rearrange("p (t d) -> p t d", t=8), self.kakeys(l, jp))
                    else:
                        ks = V(kvo[:, KV_MOBA_K + h * 1024:KV_MOBA_K + (h + 1) * 1024], self.kvkeys(l, [(KV_MOBA_K + h * 1024, 1024)]))
                        vs = V(kvo[:, KV_MOBA_V + h * 1024:KV_MOBA_V + (h + 1) * 1024].rearrange("p (t d) -> p t d", t=8), self.kvkeys(l, [(KV_MOBA_V + h * 1024, 1024)]))
                    self.dma("sp", K_[:, jp, :], ks, "jk%d_%d" % (par, jp))
                    self.dma("sp", V(V_.h[:, jp, :, :], V_[:, jp, :, :].keys), vs, "jv%d_%d" % (par, jp))
                return K_, V_

            def comp(kv, h=h):
                K_, V_ = kv
                nmT = negmT[h % 2]
                kin = V(K_.h[:, 0:4, :].rearrange("p j (b r) -> p (j b) r", b=4), K_[:, 0:4, :].keys)
                self.dve(lambda e: e.tensor_reduce(kbar[:, :].ap, kin.ap, AX.X, ALU.add), [kin], [kbar[:, :]])
                self.dve(lambda e: e.tensor_scalar(kbarb[:, :].ap, kbar[:, :].ap, 1.0 / 256.0, None, ALU.mult), [kbar[:, :]], [kbarb[:, :]])
                gp = self.bank(MISC)
                for t in range(8):
                    self.mm(gp[:, t * 16:(t + 1) * 16], self.qside[:, 4 + h, t * 128:(t + 1) * 128], kbarb[:, :], True, True)
                gv = V(gsc.h[:, :, :].rearrange("p t n -> p (t n)"), gsc[:, :, :].keys)
                gn_ = V(gnegT.h[:, :, :].rearrange("p t n -> p (t n)"), gnegT[:, :, :].keys)
                self.tt(gv, gp[:, 0:128], gn_, ALU.add)
                for t in range(8):
                    self.dve(lambda e, t=t: e.max(top8[:, t, :].ap, gsc[:, t, :].ap), [gsc[:, t, :]], [top8[:, t, :]])
                t3 = V(top8.h[:, :, 2], top8[:, :, :].keys)
                self.dve(lambda e: e.tensor_scalar(thr[:, :].ap, t3.ap, -1e29, None, ALU.max), [t3], [thr[:, :]])
                tp = None
                for t in range(8):
                    n_ = nm[t % 2]
                    self.dve(lambda e, t=t, n_=n_: e.tensor_scalar(n_[:, :].ap, gsc[:, t, :].ap, thr[:, t:t + 1].ap, -NEG, ALU.is_ge, ALU.mult),
                             [gsc[:, t, :], thr[:, t:t + 1]], [n_[:, :]])
                    self.dve(lambda e, n_=n_: e.tensor_scalar(n_[:, :].ap, n_[:, :].ap, NEG, None, ALU.add), [n_[:, :]], [n_[:, :]])
                    if t % 4 == 0:
                        tp = self.bank(MISC)
                    self.tr(tp[0:16, (t % 4) * 128:(t % 4 + 1) * 128], n_[:, :], self.idf[:, :])
                    if t % 4 == 3:
                        d = nmT[:, (t // 4) * 512:(t // 4 + 1) * 512]
                        self.S.add("act", lambda e, d=d, tp=tp: e.copy(d.ap, tp[0:16, :].ap), [tp[0:16, :]], [d])
                for qg in range(2):
                    q = self.qside[:, 4 + h, qg * 512:(qg + 1) * 512]
                    steps = []
                    for jp in (range(self.j + 1) if self.seq else range(4)):
                        for i in range(8):
                            n = 4 * jp + i // 2
                            steps.append({"qk": [(K_[:, jp, i * 128:(i + 1) * 128], q)],
                                          "mask": [(esel[0:16, n * 128:(n + 1) * 128], nmT[0:16, qg * 512:(qg + 1) * 512])],
                                          "v": V_[:, jp, i, :]})
                    for d_ in range(4):
                        i = 4 * qg + d_
                        steps.append({"qk": [(K_[:, 4, i * 128:(i + 1) * 128], q)], "mask": [ident_mask(4 + d_)], "v": V_[:, 4, i, :]})
                    run_job(steps, 128.0 ** -0.5, self.qside[:, 4 + h, qg * 512:(qg + 1) * 512])
            tasks.append((load, comp))
        for kvh in range(2):
            def load(kvh=kvh):
                par = kvh % 2
                K_, V_ = jK[par], jV[par]
                self.dma("sp", K_[:, 0, :], V(kvo[:, KV_SWA_K + kvh * 1024:KV_SWA_K + (kvh + 1) * 1024], self.kvkeys(l, [(KV_SWA_K + kvh * 1024, 1024)])), "jk%d_0" % par)
                self.dma("sp", V(V_.h[:, 0, :, :], V_[:, 0, :, :].keys),
                         V(kvo[:, KV_SWA_V + kvh * 1024:KV_SWA_V + (kvh + 1) * 1024].rearrange("p (t d) -> p t d", t=8), self.kvkeys(l, [(KV_SWA_V + kvh * 1024, 1024)])), "jv%d_0" % par)
                for jp in range(3):
                    self.dma("sp", K_[:, 1 + jp, 0:128], V(kva[jp * 128:(jp + 1) * 128, KV_SWA_K + kvh * 1024 + 896:KV_SWA_K + (kvh + 1) * 1024], self.kakeys(l, jp)), "jk%d_%d" % (par, 1 + jp))
                    self.dma("sp", V_[:, 1 + jp, 0, :], V(kva[jp * 128:(jp + 1) * 128, KV_SWA_V + kvh * 1024 + 896:KV_SWA_V + (kvh + 1) * 1024], self.kakeys(l, jp)), "jv%d_%d" % (par, 1 + jp))
                return K_, V_

            def comp(kv, kvh=kvh):
                K_, V_ = kv
                for g in range(4):
                    hq = kvh * 4 + g
                    for qg in range(2):
                        q = self.qside[:, 8 + hq, qg * 512:(qg + 1) * 512]
                        steps = []
                        if qg == 0:
                            for jp in (([self.j - 1] if self.j > 0 else []) if self.seq else range(3)):
                                steps.append({"qk": [(K_[:, 1 + jp, 0:128], q)], "mask": [ident_mask(8)], "v": V_[:, 1 + jp, 0, :],
                                              "bias": self.ngt[:, 4 + jp:5 + jp]})
                        for d_ in range(-1 if qg == 1 else 0, 4):
                            i = 4 * qg + d_
                            steps.append({"qk": [(K_[:, 0, i * 128:(i + 1) * 128], q)], "mask": [ident_mask(9 + d_)], "v": V_[:, 0, i, :]})
                        run_job(steps, 128.0 ** -0.5, self.qside[:, 8 + hq, qg * 512:(qg + 1) * 512], sink=self.esk[:, l * 8 + hq:l * 8 + hq + 1])
            tasks.append((load, comp))
        self.pipeline(tasks)

        xs = self.xs[l]
        for c4 in range(4):
            dst = V(self.xT.h[:, c4 * 4:(c4 + 1) * 4, :].rearrange("p c t -> p (c t)"), self.xT[:, c4 * 4:(c4 + 1) * 4, :].keys)
            self.dma("sp", dst, DV(xs[:, c4 * 4096:(c4 + 1) * 4096], "xs%d_%d" % (self.j if self.seq else l, c4)), "xsp%d" % c4)
        wout = self.w[("out", l)]
        tasks = []
        for ob in range(8):
            def load(ob=ob):
                w_, k_ = self.wslot([128, NCH, 256])
                self.wload(w_[:, :, :], wout[:, ob * 256:(ob + 1) * 256].rearrange("(c p) n -> p c n", p=128), k_, "wout")
                return w_

            def comp(w_, ob=ob):
                for sub in range(2):
                    o = ob * 2 + sub
                    for tg in range(2):
                        op_ = self.bank()
                        for c in range(NCH):
                            self.mm(op_[:, :], w_[:, c, sub * 128:(sub + 1) * 128], self.qside[:, c, tg * 512:(tg + 1) * 512], c == 0, c == NCH - 1)
                        xv = self.xT[:, o, tg * 512:(tg + 1) * 512]
                        self.tt(xv, op_[:, :], xv, ALU.add)
            tasks.append((load, comp))
        self.pipeline(tasks)


def _bf(a):
    return np.ascontiguousarray(a).astype(ml_dtypes.bfloat16)


def _col(g):
    return np.ascontiguousarray(g.reshape(-1, 128).T)


def static_consts():
    c = np.zeros((128, 640), np.float32)
    c[:, 0:128] = np.eye(128)
    c[:, 128:256] = 1.0
    r = np.zeros((128, 128), np.float32)
    for m in range(64):
        r[m + 64, m] = -1.0
        r[m, m + 64] = 1.0
    c[:, 256:384] = r
    r2 = np.zeros((128, 128), np.float32)
    for b in range(2):
        for m in range(32):
            r2[b * 64 + m + 32, b * 64 + m] = -1.0
            r2[b * 64 + m, b * 64 + m + 32] = 1.0
    c[:, 384:512] = r2
    p = np.arange(128)[:, None]
    f = np.arange(512)[None, :]
    tq, fl = f // 128, f % 128
    masks = np.full((128, 13, 512), NEG, np.float32)
    for d in range(4):
        vis = (tq > d) | ((tq == d) & (p <= fl))
        masks[:, d, :] = np.where(vis, 0.0, NEG)
        visb = vis & ((d // 2) == (tq // 2))
        masks[:, 4 + d, :] = np.where(visb, 0.0, NEG)
    for d in range(-1, 4):
        vis = ((tq == d) & (p <= fl)) | ((tq == d + 1) & (p > fl))
        masks[:, 9 + d, :] = np.where(vis, 0.0, NEG)
    es = np.zeros((16, 2048), np.float32)
    for n in range(16):
        es[n, n * 128:(n + 1) * 128] = 1.0
    return {"consts": _bf(c), "identf": np.eye(128, dtype=np.float32), "masks": _bf(masks), "esel": _bf(es)}


def core_consts(j):
    pos = (1024 * j + np.arange(T)).astype(np.float32)
    def tab(d):
        half = d // 2
        inv = (1.0 / (10000.0 ** (np.arange(half, dtype=np.float32) * (2.0 / d)))).astype(np.float32)
        ang = pos[None, :] * inv[:, None]
        return np.cos(ang).astype(np.float32), np.sin(ang).astype(np.float32)
    c128, s128 = tab(128)
    c64, s64 = tab(64)
    rope = np.zeros((128, 4, T), np.float32)
    rope[:, 0] = np.concatenate([c128, c128], 0)
    rope[:, 1] = np.concatenate([s128, s128], 0)
    rope[:, 2] = np.concatenate([c64, c64, c64, c64], 0)
    rope[:, 3] = np.concatenate([s64, s64, s64, s64], 0)
    negt = np.zeros((128, 8), np.float32)
    for jp in range(4):
        negt[:, jp] = 0.0 if jp < j else NEG
    for jp in range(3):
        negt[:, 4 + jp] = 0.0 if jp == j - 1 else NEG
    gneg = np.zeros((128, 8, 16), np.float32)
    for t in range(8):
        own = 4 * j + t // 2
        gneg[:, t, own:] = -1e30
    return {"rope": rope, "negt": negt, "gneg": gneg}


def gains_tensor(inp):
    g = np.zeros((128, NGC), np.float32)
    for l in range(2):
        g[:, gcol(l, "ffn1"):gcol(l, "ffn1") + 16] = _col(np.asarray(inp["ffn1_norm"][l]))
        g[:, gcol(l, "attn"):gcol(l, "attn") + 16] = _col(np.asarray(inp["attn_norm"][l]))
        g[:, gcol(l, "ffn2"):gcol(l, "ffn2") + 16] = _col(np.asarray(inp["ffn2_norm"][l]))
        g[:, gcol(l, "qn"):gcol(l, "qn") + 4] = _col(np.asarray(inp["mla_q_norm"][l]))
        g[:, gcol(l, "kvn"):gcol(l, "kvn") + 2] = _col(np.asarray(inp["mla_kv_norm"][l]))
    g[:, GFINAL:GFINAL + 16] = _col(np.asarray(inp["final_norm"]))
    return g


def common_maps(inp):
    sc = static_consts()
    gains = gains_tensor(inp)
    sinks = np.ascontiguousarray(np.broadcast_to(np.asarray(inp["swa_sinks"]).reshape(1, 16), (128, 16))).astype(np.float32)
    maps = []
    for r in range(8):
        m = dict(sc)
        m.update(core_consts(r % 4))
        m["gains"] = gains
        m["sinks"] = sinks
        maps.append(m)
    return maps


FUSED = False
SEQ = True


def _launch(stage, maps):
    nc = bass.Bass("TRN2", target_bir_lowering=False)
    Builder(nc, stage).build()
    res = run_bass_kernel_spmd(nc, maps, core_ids=list(range(8)))
    return res.results


def _wmaps(inp, m, ffn1_layers, attn_layers):
    for l in ffn1_layers:
        m["wg1_%d" % l] = inp["ffn1_w_gate"][l]
        m["wu1_%d" % l] = inp["ffn1_w_up"][l]
        m["wd1_%d" % l] = inp["ffn1_w_down"][l]
        m["win_%d" % l] = inp["w_in"][l]
        m["wuq_%d" % l] = inp["mla_w_uq"][l]
        m["wukv_%d" % l] = inp["mla_w_ukv"][l]
    for l in attn_layers:
        m["wout_%d" % l] = inp["w_out"][l]
        m["wg2_%d" % l] = inp["ffn2_w_gate"][l]
        m["wu2_%d" % l] = inp["ffn2_w_up"][l]
        m["wd2_%d" % l] = inp["ffn2_w_down"][l]


def kernel(**inputs):
    inp = {k: np.asarray(v) for k, v in inputs.items()}
    cm = common_maps(inp)
    x = np.ascontiguousarray(inp["x"].reshape(8, T, D))
    if SEQ:
        maps = []
        ccs = [core_consts(j) for j in range(4)]
        for r in range(8):
            m = dict(cm[r])
            m["rope"] = np.concatenate([c["rope"] for c in ccs], axis=1)
            m["negt"] = np.concatenate([c["negt"] for c in ccs], axis=1)
            m["gneg"] = np.concatenate([c["gneg"] for c in ccs], axis=1)
            m["x"] = np.ascontiguousarray(inp["x"][r // 4])
            _wmaps(inp, m, [0, 1], [0, 1])
            maps.append(m)
        res = _launch("SEQ", maps)
        return np.stack([res[0]["y"], res[4]["y"]]).reshape(2, 4096, D)
    if FUSED:
        maps = []
        for r in range(8):
            m = dict(cm[r])
            m["x"] = x[r]
            _wmaps(inp, m, [0, 1], [0, 1])
            maps.append(m)
        res = _launch("ALL", maps)
        return np.stack([res[r]["y"] for r in range(8)]).reshape(2, 4096, D)

    def handoff(res, maps):
        for r in range(8):
            b = r // 4
            maps[r]["xs_i"] = res[r]["xs_o"]
            maps[r]["qs_i"] = res[r]["qs_o"]
            maps[r]["kvown_i"] = res[r]["kvown_o"]
            maps[r]["kvall_i"] = np.concatenate([res[4 * b + jj]["kvown_o"] for jj in range(4)], axis=0)

    maps = []
    for r in range(8):
        m = dict(cm[r])
        m["x"] = x[r]
        _wmaps(inp, m, [0], [])
        maps.append(m)
    res = _launch("A0", maps)
    maps = []
    for r in range(8):
        m = dict(cm[r])
        _wmaps(inp, m, [1], [0])
        maps.append(m)
    handoff(res, maps)
    res = _launch("B0A1", maps)
    maps = []
    for r in range(8):
        m = dict(cm[r])
        _wmaps(inp, m, [], [1])
        maps.append(m)
    handoff(res, maps)
    res = _launch("B1F", maps)
    return np.stack([res[r]["y"] for r in range(8)]).reshape(2, 4096, D)
```

```python
import contextlib
import numpy as np
import ml_dtypes
import concourse.bass as bass
import concourse.mybir as mybir
from concourse.bass_utils import run_bass_kernel_spmd

F32 = mybir.dt.float32
BF16 = mybir.dt.bfloat16
AF = mybir.ActivationFunctionType
ALU = mybir.AluOpType
AX = mybir.AxisListType

D = 2048
DFF = 5632
T = 1024
NCH = 16
INW = 3904
EPS = 1e-6
NEG = -30000.0
GR = 256

KV_MLA_KN = 0
KV_MLA_KPE = 4096
KV_MLA_V = 5120
KV_MOBA_K = 9216
KV_MOBA_V = 13312
KV_SWA_K = 17408
KV_SWA_V = 19456
KVC = 21504
QSC = 16 * 1024 + 2 * 1024

def gcol(l, kind):
    base = l * 54
    return base + {"ffn1": 0, "attn": 16, "ffn2": 32, "qn": 48, "kvn": 52}[kind]
GFINAL = 108
NGC = 124


class V:
    __slots__ = ("ap", "keys")

    def __init__(self, ap, keys):
        self.ap = ap
        self.keys = keys


class TileT:
    def __init__(self, h, space, base, shape, esz):
        self.h, self.space, self.base, self.shape, self.esz = h, space, base, list(shape), esz
        st = [1] * len(shape)
        for i in range(len(shape) - 2, -1, -1):
            st[i] = st[i + 1] * shape[i + 1]
        self.st = st

    def __getitem__(self, idx):
        if not isinstance(idx, tuple):
            idx = (idx,)
        idx = list(idx) + [slice(None)] * (len(self.shape) - len(idx))
        lo = 0
        hi = 0
        for d in range(1, len(self.shape)):
            i = idx[d]
            if isinstance(i, slice):
                a = 0 if i.start is None else i.start
                b = self.shape[d] if i.stop is None else i.stop
            else:
                a, b = i, i + 1
            lo += a * self.st[d]
            hi += (b - 1) * self.st[d]
        lo = self.base + lo * self.esz
        hi = self.base + (hi + 1) * self.esz
        keys = [(self.space, g) for g in range(lo // GR, (hi - 1) // GR + 1)]
        return V(self.h[tuple(idx)], keys)


def DV(ap, key):
    return V(ap, [("D", key)])


class Sched:
    ENG = ("pe", "act", "dve", "pool", "sp")

    def __init__(self, nc):
        self.nc = nc
        self.ops = []
        self.last_w = {}
        self.rd_eng = {}
        self.rd_dma = {}
        self.dma_cnt = {}

    def add(self, eng, fn, reads=(), writes=(), dma_key=None):
        idx = len(self.ops)
        deps = set()
        for r in reads:
            for k in r.keys:
                w = self.last_w.get(k)
                if w is not None:
                    deps.add(w)
        for wv in writes:
            for k in wv.keys:
                w = self.last_w.get(k)
                if w is not None:
                    deps.add(w)
                d = self.rd_eng.get(k)
                if d:
                    deps.update(d.values())
                d = self.rd_dma.get(k)
                if d:
                    deps.update(d)
        is_dma = dma_key is not None
        for r in reads:
            for k in r.keys:
                if is_dma:
                    self.rd_dma.setdefault(k, []).append(idx)
                else:
                    self.rd_eng.setdefault(k, {})[eng] = idx
        for wv in writes:
            for k in wv.keys:
                self.last_w[k] = idx
                self.rd_eng[k] = {}
                self.rd_dma[k] = []
        deps.discard(idx)
        op = {"eng": eng, "fn": fn, "deps": deps, "dma": dma_key, "inc": False}
        if is_dma:
            c = self.dma_cnt.get(dma_key, 0) + 16
            self.dma_cnt[dma_key] = c
            op["val"] = c
        self.ops.append(op)
        return idx

    def emit(self, block, stack):
        nc = self.nc
        ops = self.ops
        for op in ops:
            for d in op["deps"]:
                dop = ops[d]
                if dop["dma"] is None:
                    if dop["eng"] != op["eng"] or dop["eng"] != "pe":
                        dop["inc"] = True
        cnt = {e: 0 for e in self.ENG}
        for op in ops:
            if op["dma"] is None and op["inc"]:
                cnt[op["eng"]] += 1
                op["val"] = cnt[op["eng"]]
        esem = {e: stack.enter_context(nc.semaphore("s_" + e)) for e in ("pe", "act", "dve", "pool")}
        dsem = {k: stack.enter_context(nc.semaphore("d_%s" % k)) for k in self.dma_cnt}
        streams = {e: [i for i, op in enumerate(ops) if op["eng"] == e] for e in self.ENG}

        def run(eng_name, e):
            waited = {}
            for i in streams[eng_name]:
                op = ops[i]
                need = {}
                for d in op["deps"]:
                    dop = ops[d]
                    if dop["dma"] is not None:
                        s, v = dsem[dop["dma"]], dop["val"]
                    else:
                        if dop["eng"] == eng_name and eng_name == "pe":
                            continue
                        s, v = esem[dop["eng"]], dop["val"]
                    if need.get(s.name, (None, 0))[1] < v:
                        need[s.name] = (s, v)
                for nm, (s, v) in need.items():
                    if waited.get(nm, 0) < v:
                        e.wait_ge(s, v)
                        waited[nm] = v
                ins = op["fn"](e)
                if op["dma"] is not None:
                    ins.then_inc(dsem[op["dma"]], 16)
                elif op["inc"]:
                    ins.then_inc(esem[eng_name], 1)
            if eng_name in ("sp", "pool"):
                for k, c in self.dma_cnt.items():
                    if any(ops[i]["dma"] == k for i in streams[eng_name]):
                        e.wait_ge(dsem[k], c)

        @block.tensor
        def _(e):
            run("pe", e)

        @block.scalar
        def _(e):
            run("act", e)

        @block.vector
        def _(e):
            run("dve", e)

        @block.gpsimd
        def _(e):
            run("pool", e)

        @block.sync
        def _(e):
            run("sp", e)


class Builder:
    def __init__(self, nc, stage):
        self.nc = nc
        self.stage = stage
        self.S = Sched(nc)
        self.nslot = 0
        self.bank_rr = 0
        self.uid = 0

    def sb(self, name, shape, dt, off):
        esz = 4 if dt == F32 else 2
        size = esz
        for s in shape[1:]:
            size *= s
        assert off % 32 == 0 and off + size <= 212800, (name, off, size)
        self.uid += 1
        off += 16384
        h = self.nc.alloc_sbuf_tensor_at("%s_%d" % (name, self.uid), list(shape), dt, offset=off)
        return TileT(h, "S", off, shape, esz)

    def dram(self, name, shape, dt, kind):
        return self.nc.dram_tensor(name, list(shape), dt, kind=kind).ap()

    def mm(self, out, lhsT, rhs, start, stop):
        rd = [lhsT, rhs] + ([] if start else [out])
        self.S.add("pe", lambda e: e.matmul(out.ap, lhsT.ap, rhs.ap, start=start, stop=stop), rd, [out])

    def tr(self, out, in_, ident):
        self.S.add("pe", lambda e: e.transpose(out.ap, in_.ap, ident.ap), [in_, ident], [out])

    def act(self, out, in_, func, bias=None, scale=None, extra_reads=()):
        kw = {}
        if bias is not None:
            kw["bias"] = bias.ap if isinstance(bias, V) else bias
        if scale is not None:
            kw["scale"] = scale
        rd = [in_] + list(extra_reads) + ([bias] if isinstance(bias, V) else [])
        self.S.add("act", lambda e: e.activation(out.ap, in_.ap, func, **kw), rd, [out])

    def dve(self, fn, reads, writes):
        self.S.add("dve", fn, reads, writes)

    def tt(self, out, a, b, op):
        self.dve(lambda e: e.tensor_tensor(out.ap, a.ap, b.ap, op), [a, b], [out])

    def dma(self, q, out, in_, key, **kw):
        self.S.add(q, lambda e: e.dma_start(out=out.ap, in_=in_.ap, **kw), [in_], [out], dma_key=key)

    def bank(self, banks=None):
        banks = banks if banks is not None else list(range(8))
        b = banks[self.bank_rr % len(banks)]
        self.bank_rr += 1
        return self.ps[b]

    def wslot(self, shape):
        i = self.nslot % len(self.slots)
        self.nslot += 1
        off = self.slots[i]
        size = 2
        for s in shape[1:]:
            size *= s
        assert size <= self.slot_size, (shape, size)
        return self.sb("w", shape, BF16, off), "w%d" % i

    def build(self):
        nc = self.nc
        st = self.stage
        first = st in ("A0", "ALL", "DBG1", "DBG2", "SEQ")
        last = st in ("B1F", "ALL", "DBG1", "SEQ")
        IN, OUT, INT = "ExternalInput", "ExternalOutput", "Internal"
        dr = self.dram
        self.consts = dr("consts", [128, 128 * 5], BF16, IN)
        self.identf = dr("identf", [128, 128], F32, IN)
        self.gains = dr("gains", [128, NGC], F32, IN)
        self.sinks = dr("sinks", [128, 16], F32, IN)
        self.seq = st == "SEQ"
        self.j = 0
        NJ = 4 if self.seq else 1
        self.rope = dr("rope", [128, 4 * NJ, T], F32, IN)
        self.masks = dr("masks", [128, 13, 512], BF16, IN)
        self.esel = dr("esel", [16, 2048], BF16, IN)
        self.negt = dr("negt", [128, 8 * NJ], F32, IN)
        self.gneg = dr("gneg", [128, 8 * NJ, 16], F32, IN)
        if first:
            self.x_in = dr("x", [T * NJ, D], F32, IN)
        if last:
            self.y_out = dr("y", [T, D], F32, OUT)
        nl = 2 if st == "ALL" else 1
        self.w = {}
        layers_ffn1 = {"A0": [0], "B0A1": [1], "B1F": [], "ALL": [0, 1], "DBG1": [0], "DBG2": [0], "SEQ": [0, 1]}[st]
        layers_attn = {"A0": [], "B0A1": [0], "B1F": [1], "ALL": [0, 1], "DBG1": [], "DBG2": [], "SEQ": [0, 1]}[st]
        for l in layers_ffn1:
            if st != "DBG2":
                self.w[("g1", l)] = dr("wg1_%d" % l, [D, DFF], F32, IN)
                self.w[("u1", l)] = dr("wu1_%d" % l, [D, DFF], F32, IN)
                self.w[("d1", l)] = dr("wd1_%d" % l, [DFF, D], F32, IN)
            self.w[("in", l)] = dr("win_%d" % l, [D, INW], F32, IN)
            self.w[("uq", l)] = dr("wuq_%d" % l, [512, 768], F32, IN)
            self.w[("ukv", l)] = dr("wukv_%d" % l, [256, 1024], F32, IN)
        for l in layers_attn:
            self.w[("out", l)] = dr("wout_%d" % l, [D, D], F32, IN)
            self.w[("g2", l)] = dr("wg2_%d" % l, [D, DFF], F32, IN)
            self.w[("u2", l)] = dr("wu2_%d" % l, [D, DFF], F32, IN)
            self.w[("d2", l)] = dr("wd2_%d" % l, [DFF, D], F32, IN)
        self.mine = False
        if st == "SEQ":
            self.xs_all = dr("xs_all", [512, NCH * T], F32, INT)
            self.qs_all = dr("qs_all", [512, QSC], BF16, INT)
            self.xs_j = [self.xs_all[j * 128:(j + 1) * 128, :] for j in range(4)]
            self.qs_j = [self.qs_all[j * 128:(j + 1) * 128, :] for j in range(4)]
            self.kvall = [dr("kvall%d" % l, [512, KVC], BF16, INT) for l in range(2)]
            self.kvmine = dr("kvmine", [128, KVC], BF16, INT)
            self.myidx = dr("myidx", [128, 8], mybir.dt.uint32, IN)
            self.negt_m = dr("negt_m", [128, 8], F32, IN)
            self.gneg_m = dr("gneg_m", [128, 8, 16], F32, IN)
            self.set_chunk(0)
        elif st == "ALL":
            self.xs = [dr("xs%d" % l, [128, NCH * T], F32, INT) for l in range(2)]
            self.kvown = [dr("kvown%d" % l, [128, KVC], BF16, INT) for l in range(2)]
            self.kvall = [dr("kvall%d" % l, [512, KVC], BF16, INT) for l in range(2)]
        else:
            self.xs = {}
            self.kvown = {}
            self.kvall = {}
            self.qs = {}
            for l in layers_ffn1:
                self.xs[l] = dr("xs_o", [128, NCH * T], F32, OUT)
                self.kvown[l] = dr("kvown_o", [128, KVC], BF16, OUT)
                self.qs[l] = dr("qs_o", [128, QSC], BF16, OUT)
            for l in layers_attn:
                self.xs[l] = dr("xs_i", [128, NCH * T], F32, IN)
                self.kvown[l] = dr("kvown_i", [128, KVC], BF16, IN)
                self.kvall[l] = dr("kvall_i", [512, KVC], BF16, IN)
                self.qs[l] = dr("qs_i", [128, QSC], BF16, IN)

        sb = self.sb
        self.xT = sb("xT", [128, NCH, T], F32, 0)
        self.hT = sb("hT", [128, NCH, T], BF16, 65536)
        self.actT = sb("actT", [128, 22, T], BF16, 98304)
        self.qside = sb("qside", [128, 16, T], BF16, 98304)
        self.qpe = sb("qpe", [128, 2, T], BF16, 98304 + 32768)
        self.slot_size = 11264
        self.slots = [143360 + i * 11264 for i in range(5)]
        M0 = 199680
        self.sq = [sb("sq", [128, T], BF16, M0 + i * 2048) for i in range(2)]
        self.rstd = sb("rstd", [128, T], F32, M0 + 4096)
        self.silu = [sb("silu", [128, 512], F32, M0 + 8192 + i * 2048) for i in range(1)]
        C0 = M0 + 10240
        self.cst = sb("cst", [128, 640], BF16, C0)
        self.idf = sb("idf", [128, 128], F32, C0 + 1280)
        self.gn = sb("gn", [128, NGC], F32, C0 + 1792)
        self.snk = sb("snk", [128, 16], F32, C0 + 2304)
        self.ngt = sb("ngt", [128, 8], F32, C0 + 2368)
        self.esk = sb("esk", [128, 16], F32, C0 + 2400)
        self.epst = sb("epst", [128, 8], F32, C0 + 2464)
        self.epsb = self.epst[:, 0:1]
        self.uid += 1
        hidx = nc.alloc_sbuf_tensor_at("idx_%d" % self.uid, [128, 8], mybir.dt.uint32, offset=C0 + 2496 + 16384)
        self.idx = TileT(hidx, "S", C0 + 2496 + 16384, [128, 8], 4)
        self.ident = self.cst[:, 0:128]
        self.ones = self.cst[:, 128:256]
        self.rot128 = self.cst[:, 256:384]
        self.rot64 = self.cst[:, 384:512]
        self.ps = []
        for b in range(8):
            h = nc.alloc_psum_tensor("ps%d" % b, [128, 512], F32)
            self.ps.append(TileT(h, "P", b * 2048, [128, 512], 4))

        self.dma("sp", self.cst[:, :], DV(self.consts[:, :], "consts"), "c0")
        self.dma("sp", self.idf[:, :], DV(self.identf[:, :], "identf"), "c1")
        self.dma("sp", self.gn[:, :], DV(self.gains[:, :], "gains"), "c2")
        self.dma("sp", self.snk[:, :], DV(self.sinks[:, :], "sinks"), "c3")
        if not self.seq:
            self.dma("sp", self.ngt[:, :], DV(self.negt[:, :], "negt"), "c4")
        self.act(self.esk[:, :], self.snk[:, :], AF.Exp)
        ep = self.epst[:, :]
        self.dve(lambda e: e.memset(ep.ap, EPS), [], [ep])

        if st == "SEQ":
            for l in range(2):
                for j in range(4):
                    self.set_chunk(j)
                    if l == 0:
                        self.load_x()
                    else:
                        self.move_x(True)
                    self.ffn(l, 1)
                    self.seg_win(l)
                    self.spill_q(l)
                if l == 0:
                    for j in range(4):
                        self.set_chunk(j)
                        self.restore_q(l)
                        self.seg_attn(l)
                        self.ffn(l, 2)
                        self.move_x(False)
                else:
                    self.enter_mine()
                    self.seg_attn(l)
                    self.ffn(l, 2)
                    self.final()
        elif st == "ALL":
            self.load_x()
            for l in range(2):
                self.ffn(l, 1)
                self.seg_win(l)
                self.exchange(l)
                self.seg_attn(l)
                self.ffn(l, 2)
            self.final()
        elif st == "DBG1":
            self.load_x()
            self.ffn(0, 1)
            self.final(norm=False)
        elif st == "DBG2":
            self.load_x()
            self.seg_win(0)
            self.spill_q(0)
        elif st == "A0":
            self.load_x()
            self.ffn(0, 1)
            self.seg_win(0)
            self.spill_q(0)
        elif st == "B0A1":
            self.restore_q(0)
            self.seg_attn(0)
            self.ffn(0, 2)
            self.ffn(1, 1)
            self.seg_win(1)
            self.spill_q(1)
        elif st == "B1F":
            self.restore_q(1)
            self.seg_attn(1)
            self.ffn(1, 2)
            self.final()

        with contextlib.ExitStack() as stack:
            block = stack.enter_context(nc.Block())
            self.S.emit(block, stack)
        return nc

    def set_chunk(self, j):
        self.j = j
        self.xs = {0: self.xs_j[j], 1: self.xs_j[j]}
        self.qs = {0: self.qs_j[j], 1: self.qs_j[j]}
        self.kvown = {l: self.kvall[l][j * 128:(j + 1) * 128, :] for l in range(2)}

    def move_x(self, load):
        xs = self.xs[0]
        for c4 in range(4):
            sv = V(self.xT.h[:, c4 * 4:(c4 + 1) * 4, :].rearrange("p c t -> p (c t)"), self.xT[:, c4 * 4:(c4 + 1) * 4, :].keys)
            dv = DV(xs[:, c4 * 4096:(c4 + 1) * 4096], "xs%d_%d" % (self.j, c4))
            if load:
                self.dma("sp", sv, dv, "xsp%d" % c4)
            else:
                self.dma("sp", dv, sv, "xsp%d" % c4)

    def gather(self, out, in_ap, col, rd_keys, key):
        ix = self.idx[:, col:col + 1]
        rd = V(in_ap, rd_keys)
        self.S.add("pool", lambda e: e.indirect_dma_start(out=out.ap, out_offset=None, in_=in_ap,
                                                            in_offset=bass.IndirectOffsetOnAxis(ap=ix.ap, axis=0)),
                   [rd, ix], [out], dma_key=key)

    def enter_mine(self):
        self.mine = True
        self.j = 9
        self.dma("sp", self.idx[:, :], DV(self.myidx[:, :], "myidx"), "idx")
        stage = self.sb("kvst", [128, KVC], BF16, 0)
        allk = [k for jp in range(4) for k in self.kakeys(1, jp)]
        self.gather(stage[:, :], self.kvall[1][:, :], 0, allk, "gkv")
        self.dma("sp", V(self.kvmine[:, :], self.kvkeys(1, [(0, KVC)])), stage[:, :], "gkv2")
        self.kvown = {1: self.kvmine}
        qall = self.sb("qall", [128, QSC], BF16, 98304)
        qk = [("D", "qs_a%d" % j) for j in range(4)] + [("D", "qs_b%d" % j) for j in range(4)]
        self.gather(qall[:, :], self.qs_all[:, :], 0, qk, "gq")

    def kakeys(self, l, jp):
        if self.seq:
            return [("D", "kv%d_%d" % (l, jp), g) for g in range(KVC // 512)]
        return [("D", "kvall%d" % l)]

    def load_x(self):
        stage = [self.sb("xst", [128, D], F32, 98304 + i * 8192) for i in range(2)]
        for tt in range(8):
            s = stage[tt % 2]
            self.dma("sp", s[:, :], DV(self.x_in[self.j * T + tt * 128:self.j * T + (tt + 1) * 128, :], "x"), "xst%d" % (tt % 2))
            for c4 in range(4):
                pb = self.bank()
                for k in range(4):
                    c = c4 * 4 + k
                    self.tr(pb[:, k * 128:(k + 1) * 128], s[:, c * 128:(c + 1) * 128], self.idf[:, :])
                o = self.xT[:, c4 * 4:c4 * 4 + 4, tt * 128:(tt + 1) * 128]
                src = V(pb.h[:, :].rearrange("p (k t) -> p k t", k=4), pb[:, :].keys)
                if c4 % 2 == 0:
                    self.S.add("act", lambda e, o=o, src=src: e.copy(o.ap, src.ap), [src], [o])
                else:
                    self.dve(lambda e, o=o, src=src: e.tensor_copy(o.ap, src.ap), [src], [o])

    def rmsnorm(self, src, nch, dim, g0, dst, in_place=False):
        ssq = [self.bank(), self.bank()]
        for c in range(nch):
            sq = self.sq[c % 2]
            self.act(sq[:, :], src(c), AF.Square)
            for tg in range(2):
                self.mm(ssq[tg][:, :], self.ones, sq[:, tg * 512:(tg + 1) * 512], c == 0, c == nch - 1)
        for tg in range(2):
            r = self.rstd[:, tg * 512:(tg + 1) * 512]
            self.act(r, ssq[tg][:, :], AF.Sqrt, bias=self.epsb, scale=1.0 / dim)
            self.dve(lambda e, r=r: e.reciprocal(r.ap, r.ap), [r], [r])
        for c in range(nch):
            s, d = src(c), dst(c)
            g = self.gn[:, g0 + c:g0 + c + 1]
            r = self.rstd[:, :]
            self.dve(lambda e, s=s, d=d, g=g, r=r: e.scalar_tensor_tensor(d.ap, s.ap, g.ap, r.ap, ALU.mult, ALU.mult),
                     [s, g, r], [d])

    def xnorm(self, g0):
        self.rmsnorm(lambda c: self.xT[:, c, :], NCH, D, g0, lambda c: self.hT[:, c, :])

    def wload(self, dst, src_ap, key, wkey):
        self.dma("pool", dst, DV(src_ap, wkey), key)

    def ffn(self, l, which):
        self.xnorm(gcol(l, "ffn%d" % which))
        wg, wu, wd = self.w[("g%d" % which, l)], self.w[("u%d" % which, l)], self.w[("d%d" % which, l)]
        FB = list(range(8))
        for half in range(2):
            tasks = []
            for blk in range(11):
                c0 = half * 2816 + blk * 256

                def load(c0=c0):
                    tg_, kg = self.wslot([128, NCH, 256])
                    tu_, ku = self.wslot([128, NCH, 256])
                    self.wload(tg_[:, :, :], wg[:, c0:c0 + 256].rearrange("(c p) n -> p c n", p=128), kg, "wg")
                    self.wload(tu_[:, :, :], wu[:, c0:c0 + 256].rearrange("(c p) n -> p c n", p=128), ku, "wu")
                    return tg_, tu_

                def comp(ws, blk=blk):
                    tg_, tu_ = ws
                    for sub in range(2):
                        f = blk * 2 + sub
                        for tg in range(2):
                            gp, up = self.bank(FB), self.bank(FB)
                            for c in range(NCH):
                                self.mm(gp[:, :], tg_[:, c, sub * 128:(sub + 1) * 128], self.hT[:, c, tg * 512:(tg + 1) * 512], c == 0, c == NCH - 1)
                            for c in range(NCH):
                                self.mm(up[:, :], tu_[:, c, sub * 128:(sub + 1) * 128], self.hT[:, c, tg * 512:(tg + 1) * 512], c == 0, c == NCH - 1)
                            sl = self.silu[0]
                            self.act(sl[:, :], gp[:, :], AF.Silu)
                            self.tt(self.actT[:, f, tg * 512:(tg + 1) * 512], sl[:, :], up[:, :], ALU.mult)
                tasks.append((load, comp))
            for ob in range(8):
                def load(ob=ob):
                    td, kd = self.wslot([128, 22, 256])
                    self.wload(td[:, :, :], wd[half * 2816:(half + 1) * 2816, ob * 256:(ob + 1) * 256].rearrange("(c p) n -> p c n", p=128), kd, "wd")
                    return td

                def comp(td, ob=ob):
                    for sub in range(2):
                        o = ob * 2 + sub
                        for tg in range(2):
                            op_ = self.bank(FB)
                            for fc in range(22):
                                self.mm(op_[:, :], td[:, fc, sub * 128:(sub + 1) * 128], self.actT[:, fc, tg * 512:(tg + 1) * 512], fc == 0, fc == 21)
                            xv = self.xT[:, o, tg * 512:(tg + 1) * 512]
                            self.dve(lambda e, xv=xv, op_=op_: e.scalar_tensor_tensor(xv.ap, op_[:, :].ap, 0.5, xv.ap, ALU.mult, ALU.add),
                                     [op_[:, :], xv], [xv])
                tasks.append((load, comp))
            self.pipeline(tasks)

    def pipeline(self, tasks):
        cur = tasks[0][0]()
        for i in range(len(tasks)):
            nxt = tasks[i + 1][0]() if i + 1 < len(tasks) else None
            tasks[i][1](cur)
            cur = nxt

    def final(self, norm=True):
        if norm:
            self.rmsnorm(lambda c: self.xT[:, c, :], NCH, D, GFINAL, lambda c: self.xT[:, c, :])
        stage = [self.sb("yst", [128, D], F32, 65536 + i * 8192) for i in range(2)]
        for tt in range(8):
            s = stage[tt % 2]
            for c4 in range(4):
                pb = self.bank()
                for k in range(4):
                    c = c4 * 4 + k
                    self.tr(pb[:, k * 128:(k + 1) * 128], self.xT[:, c, tt * 128:(tt + 1) * 128], self.idf[:, :])
                o = s[:, c4 * 512:(c4 + 1) * 512]
                if c4 % 2 == 0:
                    self.S.add("act", lambda e, o=o, pb=pb: e.copy(o.ap, pb[:, :].ap), [pb[:, :]], [o])
                else:
                    self.dve(lambda e, o=o, pb=pb: e.tensor_copy(o.ap, pb[:, :].ap), [pb[:, :]], [o])
            self.dma("sp", DV(self.y_out[tt * 128:(tt + 1) * 128, :], "y%d" % tt), s[:, :], "yst%d" % (tt % 2))

    def kvkeys(self, l, spans):
        keys = []
        for (a, n) in spans:
            for g in range(a // 512, (a + n - 1) // 512 + 1):
                k = ("D", "kv%d_%d" % (l, self.j), g) if self.seq else ("D", "kvown%d" % l, g)
                if k not in keys:
                    keys.append(k)
        return keys

    def seg_win(self, l):
        sb = self.sb
        self.xnorm(gcol(l, "attn"))
        xs = self.xs[l]
        for c4 in range(4):
            src = V(self.xT.h[:, c4 * 4:(c4 + 1) * 4, :].rearrange("p c t -> p (c t)"), self.xT[:, c4 * 4:(c4 + 1) * 4, :].keys)
            self.dma("sp", DV(xs[:, c4 * 4096:(c4 + 1) * 4096], "xs%d_%d" % (self.j if self.seq else l, c4)), src, "xsp%d" % c4)
        ropeT = sb("ropeT", [128, 4, T], F32, 0)
        cq = sb("cq", [128, 4, T], F32, 16384)
        ckv = sb("ckv", [128, 2, T], F32, 32768)
        cqn = sb("cqn", [128, 4, T], BF16, 40960)
        ckvn = sb("ckvn", [128, 2, T], BF16, 49152)
        t1 = [sb("t1", [128, 512], F32, 53248 + i * 2048) for i in range(2)]
        t2 = [sb("t2", [128, 512], F32, 57344 + i * 2048) for i in range(2)]
        zb = [sb("zb", [128, 512], BF16, 61440 + i * 1024) for i in range(2)]
        kst = [sb("kst", [128, 512], BF16, 63488 + i * 1024) for i in range(2)]
        for i in range(4):
            self.dma("sp", ropeT[:, i, :], DV(self.rope[:, self.j * 4 + i, :], "rope"), "rp%d" % i)
        win, wuq, wukv = self.w[("in", l)], self.w[("uq", l)], self.w[("ukv", l)]
        kvo = self.kvown[l]
        cnt = {"r": 0, "k": 0}

        import os
        DBGF = int(os.environ.get("DBGF", "0"))

        def kv_store(stage_v, col, n):
            if DBGF & 1:
                return
            self.dma("sp", V(kvo[:, col:col + n], self.kvkeys(l, [(col, n)])), stage_v, "kst%d" % (cnt["k"] % 2))

        def rope_ep(zp, rp, ti, tg, dst):
            i = cnt["r"] % 2
            cnt["r"] += 1
            self.tt(t1[i][:, :], ropeT[:, ti, tg * 512:(tg + 1) * 512], zp[:, :], ALU.mult)
            self.tt(t2[i][:, :], ropeT[:, ti + 1, tg * 512:(tg + 1) * 512], rp[:, :], ALU.mult)
            self.tt(dst, t1[i][:, :], t2[i][:, :], ALU.add)

        def make_rot(w_, c0, ncols, half, nch):
            w_r, _ = self.wslot([128, nch, 256])
            src = w_.h[:, :, c0:c0 + ncols].rearrange("p c (b t r) -> p c b t r", t=2, r=half)
            dst = w_r.h[:, :, 0:ncols].rearrange("p c (b t r) -> p c b t r", t=2, r=half)
            sk, dk = w_[:, :, c0:c0 + ncols].keys, w_r[:, :, 0:ncols].keys
            a0, b1 = V(dst[:, :, :, 0, :], dk), V(src[:, :, :, 1, :], sk)
            a1, b0 = V(dst[:, :, :, 1, :], dk), V(src[:, :, :, 0, :], sk)
            self.dve(lambda e: e.tensor_scalar(a0.ap, b1.ap, -1.0, None, ALU.mult), [b1], [a0])
            self.dve(lambda e: e.tensor_copy(a1.ap, b0.ap), [b0], [a1])
            return w_r

        def fm_block(col0, ncols, dup=False, half=None):
            def load():
                w_, k_ = self.wslot([128, NCH, 256])
                if dup:
                    self.wload(w_[:, :, 0:128], win[:, col0:col0 + 128].rearrange("(c p) n -> p c n", p=128), k_, "win")
                    a, b = w_[:, :, 64:128], w_[:, :, 0:64]
                    self.dve(lambda e: e.tensor_copy(a.ap, b.ap), [b], [a])
                    nc_ = 128
                else:
                    self.wload(w_[:, :, 0:ncols], win[:, col0:col0 + ncols].rearrange("(c p) n -> p c n", p=128), k_, "win")
                    nc_ = ncols
                if half is not None:
                    return w_, make_rot(w_, 0, nc_, half, NCH)
                return w_, None
            return load

        def fm_comp(ntile, ep):
            def comp(ws):
                w_, w_r = ws
                for s_ in range(ntile):
                    for tg in range(2):
                        zp = self.bank()
                        for c in range(NCH):
                            self.mm(zp[:, :], w_[:, c, s_ * 128:(s_ + 1) * 128], self.hT[:, c, tg * 512:(tg + 1) * 512], c == 0, c == NCH - 1)
                        rp = None
                        if w_r is not None:
                            rp = self.bank()
                            for c in range(NCH):
                                self.mm(rp[:, :], w_r[:, c, s_ * 128:(s_ + 1) * 128], self.hT[:, c, tg * 512:(tg + 1) * 512], c == 0, c == NCH - 1)
                        ep(s_, tg, zp, rp)
            return comp

        def ep_copy(dst_t, c0):
            def ep(s_, tg, zp, rp):
                d = dst_t[:, c0 + s_, tg * 512:(tg + 1) * 512]
                self.S.add("act", lambda e: e.copy(d.ap, zp[:, :].ap), [zp[:, :]], [d])
            return ep

        def ep_rope_q(ch0):
            def ep(s_, tg, zp, rp):
                rope_ep(zp, rp, 0, tg, self.qside[:, ch0 + s_, tg * 512:(tg + 1) * 512])
            return ep

        def ep_rope_k(col_base, ti):
            def ep(s_, tg, zp, rp):
                st_ = kst[cnt["k"] % 2]
                rope_ep(zp, rp, ti, tg, st_[:, :])
                kv_store(st_[:, :], col_base + s_ * 1024 + tg * 512, 512)
                cnt["k"] += 1
            return ep

        def v_comp(col_base, nh):
            def comp(ws):
                w_ = ws[0]
                for tt in range(8):
                    vp = self.bank()
                    for c in range(NCH):
                        self.mm(vp[:, 0:nh * 128], self.hT[:, c, tt * 128:(tt + 1) * 128], w_[:, c, 0:nh * 128], c == 0, c == NCH - 1)
                    st_ = kst[cnt["k"] % 2]
                    sv = st_[:, 0:nh * 128]
                    self.S.add("act", lambda e, sv=sv, vp=vp: e.copy(sv.ap, vp[:, 0:nh * 128].ap), [vp[:, 0:nh * 128]], [sv])
                    dst = kvo[:, col_base:col_base + nh * 1024].rearrange("p (h r) -> p h r", h=nh)[:, :, tt * 128:(tt + 1) * 128]
                    keys = self.kvkeys(l, [(col_base + h_ * 1024 + tt * 128, 128) for h_ in range(nh)])
                    src = V(st_.h[:, 0:nh * 128].rearrange("p (h r) -> p h r", h=nh), sv.keys)
                    self.dma("sp", V(dst, keys), src, "kst%d" % (cnt["k"] % 2))
                    cnt["k"] += 1
            return comp

        def uq_load():
            w_, k_ = self.wslot([128, 4, 1024])
            self.wload(w_[:, :, 0:768], wuq[:, :].rearrange("(c p) n -> p c n", p=128), k_, "wuq")
            for c in range(4):
                dst = V(w_.h[:, c, 768:1024].rearrange("p (h r) -> p h r", h=4), w_[:, c, 768:1024].keys)
                self.wload(dst, wuq[c * 128:(c + 1) * 128, :].rearrange("p (h r) -> p h r", h=4)[:, :, 128:192], "uqpe%d" % c, "wuq")
            return w_, make_rot(w_, 768, 256, 32, 4)

        def uq_comp(ws):
            w_, w_r = ws
            self.rmsnorm(lambda c: cq[:, c, :], 4, 512, gcol(l, "qn"), lambda c: cqn[:, c, :])
            for h in range(4):
                for tg in range(2):
                    zp = self.bank()
                    for c in range(4):
                        self.mm(zp[:, :], w_[:, c, h * 192:h * 192 + 128], cqn[:, c, tg * 512:(tg + 1) * 512], c == 0, c == 3)
                    d = self.qside[:, h, tg * 512:(tg + 1) * 512]
                    self.S.add("act", lambda e, d=d, zp=zp: e.copy(d.ap, zp[:, :].ap), [zp[:, :]], [d])
            for pr in range(2):
                for tg in range(2):
                    zp, rp = self.bank(), self.bank()
                    for c in range(4):
                        lw = w_[:, c, 768 + pr * 128:768 + (pr + 1) * 128]
                        self.mm(zp[:, :], lw, cqn[:, c, tg * 512:(tg + 1) * 512], c == 0, c == 3)
                    for c in range(4):
                        self.mm(rp[:, :], w_r[:, c, pr * 128:(pr + 1) * 128], cqn[:, c, tg * 512:(tg + 1) * 512], c == 0, c == 3)
                    rope_ep(zp, rp, 2, tg, self.qpe[:, pr, tg * 512:(tg + 1) * 512])

        def ukv_load():
            w_, k_ = self.wslot([128, 2, 1024])
            self.wload(w_[:, :, :], wukv[:, :].rearrange("(c p) n -> p c n", p=128), k_, "wukv")
            return w_

        def ukv_comp(w_):
            self.rmsnorm(lambda c: ckv[:, c, :], 2, 256, gcol(l, "kvn"), lambda c: ckvn[:, c, :])
            for h in range(4):
                for tg in range(2):
                    zp = self.bank()
                    for c in range(2):
                        self.mm(zp[:, :], w_[:, c, h * 256:h * 256 + 128], ckvn[:, c, tg * 512:(tg + 1) * 512], c == 0, c == 1)
                    st_ = kst[cnt["k"] % 2]
                    self.S.add("act", lambda e, st_=st_, zp=zp: e.copy(st_[:, :].ap, zp[:, :].ap), [zp[:, :]], [st_[:, :]])
                    kv_store(st_[:, :], KV_MLA_KN + h * 1024 + tg * 512, 512)
                    cnt["k"] += 1
            for tt in range(8):
                vp = self.bank()
                for c in range(2):
                    rw = V(w_.h[:, c, :].rearrange("p (h r) -> p h r", h=4)[:, :, 128:256], w_[:, c, :].keys)
                    self.mm(vp[:, :], ckvn[:, c, tt * 128:(tt + 1) * 128], rw, c == 0, c == 1)
                st_ = kst[cnt["k"] % 2]
                self.S.add("act", lambda e, st_=st_, vp=vp: e.copy(st_[:, :].ap, vp[:, :].ap), [vp[:, :]], [st_[:, :]])
                dst = kvo[:, KV_MLA_V:KV_MLA_V + 4096].rearrange("p (h r) -> p h r", h=4)[:, :, tt * 128:(tt + 1) * 128]
                keys = self.kvkeys(l, [(KV_MLA_V + h_ * 1024 + tt * 128, 128) for h_ in range(4)])
                src = V(st_.h[:, :].rearrange("p (h r) -> p h r", h=4), st_[:, :].keys)
                self.dma("sp", V(dst, keys), src, "kst%d" % (cnt["k"] % 2))
                cnt["k"] += 1

        tasks = []
        tasks.append((fm_block(0, 256), fm_comp(2, ep_copy(cq, 0))))
        tasks.append((fm_block(256, 256), fm_comp(2, ep_copy(cq, 2))))
        tasks.append((fm_block(512, 256), fm_comp(2, ep_copy(ckv, 0))))
        tasks.append((fm_block(768, 64, dup=True, half=32), fm_comp(1, ep_rope_k(KV_MLA_KPE, 2))))
        tasks.append((uq_load, uq_comp))
        tasks.append((ukv_load, ukv_comp))
        for b in range(2):
            tasks.append((fm_block(832 + 256 * b, 256, half=64), fm_comp(2, ep_rope_q(4 + 2 * b))))
        for b in range(2):
            tasks.append((fm_block(1344 + 256 * b, 256, half=64), fm_comp(2, ep_rope_k(KV_MOBA_K + 2 * b * 1024, 0))))
        for b in range(4):
            tasks.append((fm_block(2368 + 256 * b, 256, half=64), fm_comp(2, ep_rope_q(8 + 2 * b))))
        tasks.append((fm_block(3392, 256, half=64), fm_comp(2, ep_rope_k(KV_SWA_K, 0))))
        for b in range(2):
            tasks.append((fm_block(1856 + 256 * b, 256), v_comp(KV_MOBA_V + 2 * b * 1024, 2)))
        tasks.append((fm_block(3648, 256), v_comp(KV_SWA_V, 2)))
        import os
        nt = int(os.environ.get("WIN_NT", "99"))
        self.pipeline(tasks[:nt])

    def kv_all_keys(self, l):
        return [("D", "kvown%d" % l, g) for g in range(KVC // 512)]

    def spill_q(self, l):
        qs = self.qs[l]
        src = V(self.qside.h[:, :, :].rearrange("p c t -> p (c t)"), self.qside[:, :, :].keys)
        self.dma("sp", DV(qs[:, 0:16384], "qs_a%d" % self.j), src, "qsp0")
        src = V(self.qpe.h[:, :, :].rearrange("p c t -> p (c t)"), self.qpe[:, :, :].keys)
        self.dma("sp", DV(qs[:, 16384:QSC], "qs_b%d" % self.j), src, "qsp1")

    def restore_q(self, l):
        qs = self.qs[l]
        dst = V(self.qside.h[:, :, :].rearrange("p c t -> p (c t)"), self.qside[:, :, :].keys)
        self.dma("sp", dst, DV(qs[:, 0:16384], "qs_a%d" % self.j), "qsp0")
        dst = V(self.qpe.h[:, :, :].rearrange("p c t -> p (c t)"), self.qpe[:, :, :].keys)
        self.dma("sp", dst, DV(qs[:, 16384:QSC], "qs_b%d" % self.j), "qsp1")

    def exchange(self, l):
        kvo, kva = self.kvown[l], self.kvall[l]
        rd = V(kvo[:, :], self.kv_all_keys(l))
        wr = DV(kva[:, :], "kvall%d" % l)
        self.S.add("pool", lambda e: e.collective_compute("AllGather", ALU.bypass, replica_groups=[[0, 1, 2, 3], [4, 5, 6, 7]],
                                                          ins=[kvo[:, :]], outs=[kva[:, :]]),
                   [rd], [wr], dma_key="cc%d" % l)

    def seg_attn(self, l):
        sb = self.sb
        kvo, kva = self.kvown[l], self.kvall[l]
        KA = "kvall%d" % l
        jK = [sb("jK", [128, 5, T], BF16, i * 20480) for i in range(2)]
        jV = [sb("jV", [128, 5, 8, 128], BF16, i * 20480 + 10240) for i in range(2)]
        kpe = sb("kpe", [128, 4, T], BF16, 40960)
        mk = sb("mk", [128, 13, 512], BF16, 49152)
        esel = sb("esel", [16, 2048], BF16, 62464)
        negmT = [sb("negmT", [16, T], BF16, 66560 + i * 2048) for i in range(2)]
        PT = [sb("PT", [128, 512], BF16, 70656 + i * 1024) for i in range(4)]
        rs = [sb("rs", [128, 512], F32, 74752 + i * 2048) for i in range(2)]
        gsc = sb("gsc", [128, 8, 16], F32, 78848)
        top8 = sb("top8", [128, 8, 8], F32, 79360)
        thr = sb("thr", [128, 8], F32, 79616)
        nm = [sb("nm", [128, 16], F32, 79648 + i * 64) for i in range(2)]
        kbar = sb("kbar", [128, 16], F32, 79776)
        kbarb = sb("kbarb", [128, 16], BF16, 79840)
        gnegT = sb("gnegT", [128, 8, 16], F32, 79872)
        SB_ = [4, 5]
        MISC = [6, 7]
        st = {"pt": 0, "rs": 0, "job": 0}

        for m0 in (0, 4, 8):
            n = 5 if m0 == 8 else 4
            self.dma("sp", mk[:, m0:m0 + n, :], DV(self.masks[:, m0:m0 + n, :], "masks"), "mk%d" % m0)
        self.dma("sp", esel[:, :], DV(self.esel[:, :], "esel"), "esel")
        if self.mine:
            self.dma("sp", V(gnegT.h[:, :, :], gnegT[:, :, :].keys), DV(self.gneg_m[:, :, :], "gneg_m"), "gneg")
            self.dma("sp", self.ngt[:, :], DV(self.negt_m[:, :], "negt_m"), "c4")
        else:
            self.dma("sp", V(gnegT.h[:, :, :], gnegT[:, :, :].keys), DV(self.gneg[:, self.j * 8:(self.j + 1) * 8, :], "gneg"), "gneg")
            if self.seq:
                self.dma("sp", self.ngt[:, :], DV(self.negt[:, self.j * 8:(self.j + 1) * 8], "negt"), "c4")
        for jp in range(3):
            self.dma("sp", kpe[:, jp, :], V(kva[jp * 128:(jp + 1) * 128, KV_MLA_KPE:KV_MLA_KPE + 1024], self.kakeys(l, jp)), "kpe%d" % jp)
        self.dma("sp", kpe[:, 3, :], V(kvo[:, KV_MLA_KPE:KV_MLA_KPE + 1024], self.kvkeys(l, [(KV_MLA_KPE, 1024)])), "kpe3")

        def run_job(steps, scale, outv, sink=None):
            par = st["job"] % 2
            st["job"] += 1
            acc, ssum = self.ps[2 * par], self.ps[2 * par + 1]
            nst = len(steps)
            sbanks = [None] * nst

            def qk(i):
                sbanks[i] = self.bank(SB_)
                mms = steps[i]["qk"] + steps[i].get("mask", [])
                for k, (a, b) in enumerate(mms):
                    self.mm(sbanks[i][:, :], a, b, k == 0, k == len(mms) - 1)
            qk(0)
            for i in range(nst):
                if i + 1 < nst:
                    qk(i + 1)
                pt = PT[st["pt"] % 4]
                st["pt"] += 1
                self.act(pt[:, :], sbanks[i][:, :], AF.Exp, bias=steps[i].get("bias"), scale=scale)
                self.mm(acc[:, :], steps[i]["v"], pt[:, :], i == 0, i == nst - 1)
                self.mm(ssum[:, :], self.ones, pt[:, :], i == 0, i == nst - 1)
            r = rs[st["rs"] % 2]
            st["rs"] += 1
            if sink is not None:
                self.dve(lambda e: e.tensor_scalar(r[:, :].ap, ssum[:, :].ap, sink.ap, None, ALU.add), [ssum[:, :], sink], [r[:, :]])
                self.dve(lambda e: e.reciprocal(r[:, :].ap, r[:, :].ap), [r[:, :]], [r[:, :]])
            else:
                self.dve(lambda e: e.reciprocal(r[:, :].ap, ssum[:, :].ap), [ssum[:, :]], [r[:, :]])
            self.tt(outv, acc[:, :], r[:, :], ALU.mult)

        def ident_mask(m):
            return (self.ident, mk[:, m, :])

        tasks = []
        for h in range(4):
            def load(h=h):
                par = h % 2
                K_, V_ = jK[par], jV[par]
                for jp in (range(self.j) if (self.seq and not self.mine) else range(3)):
                    self.dma("sp", K_[:, jp, :], V(kva[jp * 128:(jp + 1) * 128, KV_MLA_KN + h * 1024:KV_MLA_KN + (h + 1) * 1024], self.kakeys(l, jp)), "jk%d_%d" % (par, jp))
                    self.dma("sp", V(V_.h[:, jp, :, :], V_[:, jp, :, :].keys),
                             V(kva[jp * 128:(jp + 1) * 128, KV_MLA_V + h * 1024:KV_MLA_V + (h + 1) * 1024].rearrange("p (t d) -> p t d", t=8), self.kakeys(l, jp)), "jv%d_%d" % (par, jp))
                self.dma("sp", K_[:, 3, :], V(kvo[:, KV_MLA_KN + h * 1024:KV_MLA_KN + (h + 1) * 1024], self.kvkeys(l, [(KV_MLA_KN + h * 1024, 1024)])), "jk%d_3" % par)
                self.dma("sp", V(V_.h[:, 3, :, :], V_[:, 3, :, :].keys),
                         V(kvo[:, KV_MLA_V + h * 1024:KV_MLA_V + (h + 1) * 1024].rearrange("p (t d) -> p t d", t=8), self.kvkeys(l, [(KV_MLA_V + h * 1024, 1024)])), "jv%d_3" % par)
                return K_, V_

            def comp(kv, h=h):
                K_, V_ = kv
                p0 = (h % 2) * 64
                for qg in range(2):
                    qn = self.qside[:, h, qg * 512:(qg + 1) * 512]
                    qp = self.qpe[p0:p0 + 64, h // 2, qg * 512:(qg + 1) * 512]
                    steps = []
                    for jp in (range(self.j) if (self.seq and not self.mine) else range(3)):
                        for i in range(8):
                            steps.append({"qk": [(K_[:, jp, i * 128:(i + 1) * 128], qn), (kpe[p0:p0 + 64, jp, i * 128:(i + 1) * 128], qp)],
                                          "v": V_[:, jp, i, :], "bias": self.ngt[:, jp:jp + 1]})
                    for i in range(4 * qg + 4):
                        s_ = {"qk": [(K_[:, 3, i * 128:(i + 1) * 128], qn), (kpe[p0:p0 + 64, 3, i * 128:(i + 1) * 128], qp)], "v": V_[:, 3, i, :]}
                        if i >= 4 * qg:
                            s_["mask"] = [ident_mask(i - 4 * qg)]
                        steps.append(s_)
                    run_job(steps, 192.0 ** -0.5, self.qside[:, h, qg * 512:(qg + 1) * 512])
            tasks.append((load, comp))
        for h in range(4):
            def load(h=h):
                par = h % 2
                K_, V_ = jK[par], jV[par]
                for jp in range(5):
                    if jp < 4:
                        ks = V(kva[jp * 128:(jp + 1) * 128, KV_MOBA_K + h * 1024:KV_MOBA_K + (h + 1) * 1024], self.kakeys(l, jp))
                        vs = V(kva[jp * 128:(jp + 1) * 128, KV_MOBA_V + h * 1024:KV_MOBA_V + (h + 1) * 1024].rearrange("p (t d) -> p t d", t=8), self.kakeys(l, jp))
                    else:
                        ks = V(kvo[:, KV_MOBA_K + h * 1024:KV_MOBA_K + (h + 1) * 1024], self.kvkeys(l, [(KV_MOBA_K + h * 1024, 1024)]))
                        vs = V(kvo[:, KV_MOBA_V + h * 1024:KV_MOBA_V + (h + 1) * 1024].rearrange("p (t d) -> p t d", t=8), self.kvkeys(l, [(KV_MOBA_V + h * 1024, 1024)]))
                    self.dma("sp", K_[:, jp, :], ks, "jk%d_%d" % (par, jp))
                    self.dma("sp", V(V_.h[:, jp, :, :], V_[:, jp, :, :].keys), vs, "jv%d_%d" % (par, jp))
                return K_, V_

            def comp(kv, h=h):
                K_, V_ = kv
                nmT = negmT[h % 2]
                kin = V(K_.h[:, 0:4, :].rearrange("p j (b r) -> p (j b) r", b=4), K_[:, 0:4, :].keys)
                self.dve(lambda e: e.tensor_reduce(kbar[:, :].ap, kin.ap, AX.X, ALU.add), [kin], [kbar[:, :]])
                self.dve(lambda e: e.tensor_scalar(kbarb[:, :].ap, kbar[:, :].ap, 1.0 / 256.0, None, ALU.mult), [kbar[:, :]], [kbarb[:, :]])
                gp = self.bank(MISC)
                for t in range(8):
                    self.mm(gp[:, t * 16:(t + 1) * 16], self.qside[:, 4 + h, t * 128:(t + 1) * 128], kbarb[:, :], True, True)
                gv = V(gsc.h[:, :, :].rearrange("p t n -> p (t n)"), gsc[:, :, :].keys)
                gn_ = V(gnegT.h[:, :, :].rearrange("p t n -> p (t n)"), gnegT[:, :, :].keys)
                self.tt(gv, gp[:, 0:128], gn_, ALU.add)
                for t in range(8):
                    self.dve(lambda e, t=t: e.max(top8[:, t, :].ap, gsc[:, t, :].ap), [gsc[:, t, :]], [top8[:, t, :]])
                t3 = V(top8.h[:, :, 2], top8[:, :, :].keys)
                self.dve(lambda e: e.tensor_scalar(thr[:, :].ap, t3.ap, -1e29, None, ALU.max), [t3], [thr[:, :]])
                tp = None
                for t in range(8):
                    n_ = nm[t % 2]
                    self.dve(lambda e, t=t, n_=n_: e.tensor_scalar(n_[:, :].ap, gsc[:, t, :].ap, thr[:, t:t + 1].ap, -NEG, ALU.is_ge, ALU.mult),
                             [gsc[:, t, :], thr[:, t:t + 1]], [n_[:, :]])
                    self.dve(lambda e, n_=n_: e.tensor_scalar(n_[:, :].ap, n_[:, :].ap, NEG, None, ALU.add), [n_[:, :]], [n_[:, :]])
                    if t % 4 == 0:
                        tp = self.bank(MISC)
                    self.tr(tp[0:16, (t % 4) * 128:(t % 4 + 1) * 128], n_[:, :], self.idf[:, :])
                    if t % 4 == 3:
                        d = nmT[:, (t // 4) * 512:(t // 4 + 1) * 512]
                        self.S.add("act", lambda e, d=d, tp=tp: e.copy(d.ap, tp[0:16, :].ap), [tp[0:16, :]], [d])
                for qg in range(2):
                    q = self.qside[:, 4 + h, qg * 512:(qg + 1) * 512]
                    steps = []
                    for jp in (range(self.j + 1) if (self.seq and not self.mine) else range(4)):
                        for i in range(8):
                            n = 4 * jp + i // 2
                            steps.append({"qk": [(K_[:, jp, i * 128:(i + 1) * 128], q)],
                                          "mask": [(esel[0:16, n * 128:(n + 1) * 128], nmT[0:16, qg * 512:(qg + 1) * 512])],
                                          "v": V_[:, jp, i, :]})
                    for d_ in range(4):
                        i = 4 * qg + d_
                        steps.append({"qk": [(K_[:, 4, i * 128:(i + 1) * 128], q)], "mask": [ident_mask(4 + d_)], "v": V_[:, 4, i, :]})
                    run_job(steps, 128.0 ** -0.5, self.qside[:, 4 + h, qg * 512:(qg + 1) * 512])
            tasks.append((load, comp))
        for kvh in range(2):
            def load(kvh=kvh):
                par = kvh % 2
                K_, V_ = jK[par], jV[par]
                self.dma("sp", K_[:, 0, :], V(kvo[:, KV_SWA_K + kvh * 1024:KV_SWA_K + (kvh + 1) * 1024], self.kvkeys(l, [(KV_SWA_K + kvh * 1024, 1024)])), "jk%d_0" % par)
                self.dma("sp", V(V_.h[:, 0, :, :], V_[:, 0, :, :].keys),
                         V(kvo[:, KV_SWA_V + kvh * 1024:KV_SWA_V + (kvh + 1) * 1024].rearrange("p (t d) -> p t d", t=8), self.kvkeys(l, [(KV_SWA_V + kvh * 1024, 1024)])), "jv%d_0" % par)
                for jp in range(3):
                    self.dma("sp", K_[:, 1 + jp, 0:128], V(kva[jp * 128:(jp + 1) * 128, KV_SWA_K + kvh * 1024 + 896:KV_SWA_K + (kvh + 1) * 1024], self.kakeys(l, jp)), "jk%d_%d" % (par, 1 + jp))
                    self.dma("sp", V_[:, 1 + jp, 0, :], V(kva[jp * 128:(jp + 1) * 128, KV_SWA_V + kvh * 1024 + 896:KV_SWA_V + (kvh + 1) * 1024], self.kakeys(l, jp)), "jv%d_%d" % (par, 1 + jp))
                return K_, V_

            def comp(kv, kvh=kvh):
                K_, V_ = kv
                for g in range(4):
                    hq = kvh * 4 + g
                    for qg in range(2):
                        q = self.qside[:, 8 + hq, qg * 512:(qg + 1) * 512]
                        steps = []
                        if qg == 0:
                            for jp in (([self.j - 1] if self.j > 0 else []) if (self.seq and not self.mine) else range(3)):
                                steps.append({"qk": [(K_[:, 1 + jp, 0:128], q)], "mask": [ident_mask(8)], "v": V_[:, 1 + jp, 0, :],
                                              "bias": self.ngt[:, 4 + jp:5 + jp]})
                        for d_ in range(-1 if qg == 1 else 0, 4):
                            i = 4 * qg + d_
                            steps.append({"qk": [(K_[:, 0, i * 128:(i + 1) * 128], q)], "mask": [ident_mask(9 + d_)], "v": V_[:, 0, i, :]})
                        run_job(steps, 128.0 ** -0.5, self.qside[:, 8 + hq, qg * 512:(qg + 1) * 512], sink=self.esk[:, l * 8 + hq:l * 8 + hq + 1])
            tasks.append((load, comp))
        self.pipeline(tasks)

        xs = self.xs[l]
        for c4 in range(4):
            dst = V(self.xT.h[:, c4 * 4:(c4 + 1) * 4, :].rearrange("p c t -> p (c t)"), self.xT[:, c4 * 4:(c4 + 1) * 4, :].keys)
            if self.mine:
                xk = [("D", "xs%d_%d" % (j, c4)) for j in range(4)]
                self.gather(dst, self.xs_all[:, :].rearrange("r (c n) -> (r c) n", c=4), 1 + c4, xk, "gx%d" % c4)
            else:
                self.dma("sp", dst, DV(xs[:, c4 * 4096:(c4 + 1) * 4096], "xs%d_%d" % (self.j if self.seq else l, c4)), "xsp%d" % c4)
        wout = self.w[("out", l)]
        tasks = []
        for ob in range(8):
            def load(ob=ob):
                w_, k_ = self.wslot([128, NCH, 256])
                self.wload(w_[:, :, :], wout[:, ob * 256:(ob + 1) * 256].rearrange("(c p) n -> p c n", p=128), k_, "wout")
                return w_

            def comp(w_, ob=ob):
                for sub in range(2):
                    o = ob * 2 + sub
                    for tg in range(2):
                        op_ = self.bank()
                        for c in range(NCH):
                            self.mm(op_[:, :], w_[:, c, sub * 128:(sub + 1) * 128], self.qside[:, c, tg * 512:(tg + 1) * 512], c == 0, c == NCH - 1)
                        xv = self.xT[:, o, tg * 512:(tg + 1) * 512]
                        self.tt(xv, op_[:, :], xv, ALU.add)
            tasks.append((load, comp))
        self.pipeline(tasks)


def _bf(a):
    return np.ascontiguousarray(a).astype(ml_dtypes.bfloat16)


def _col(g):
    return np.ascontiguousarray(g.reshape(-1, 128).T)


def static_consts():
    c = np.zeros((128, 640), np.float32)
    c[:, 0:128] = np.eye(128)
    c[:, 128:256] = 1.0
    r = np.zeros((128, 128), np.float32)
    for m in range(64):
        r[m + 64, m] = -1.0
        r[m, m + 64] = 1.0
    c[:, 256:384] = r
    r2 = np.zeros((128, 128), np.float32)
    for b in range(2):
        for m in range(32):
            r2[b * 64 + m + 32, b * 64 + m] = -1.0
            r2[b * 64 + m, b * 64 + m + 32] = 1.0
    c[:, 384:512] = r2
    p = np.arange(128)[:, None]
    f = np.arange(512)[None, :]
    tq, fl = f // 128, f % 128
    masks = np.full((128, 13, 512), NEG, np.float32)
    for d in range(4):
        vis = (tq > d) | ((tq == d) & (p <= fl))
        masks[:, d, :] = np.where(vis, 0.0, NEG)
        visb = vis & ((d // 2) == (tq // 2))
        masks[:, 4 + d, :] = np.where(visb, 0.0, NEG)
    for d in range(-1, 4):
        vis = ((tq == d) & (p <= fl)) | ((tq == d + 1) & (p > fl))
        masks[:, 9 + d, :] = np.where(vis, 0.0, NEG)
    es = np.zeros((16, 2048), np.float32)
    for n in range(16):
        es[n, n * 128:(n + 1) * 128] = 1.0
    return {"consts": _bf(c), "identf": np.eye(128, dtype=np.float32), "masks": _bf(masks), "esel": _bf(es)}


def core_consts(j):
    pos = (1024 * j + np.arange(T)).astype(np.float32)
    def tab(d):
        half = d // 2
        inv = (1.0 / (10000.0 ** (np.arange(half, dtype=np.float32) * (2.0 / d)))).astype(np.float32)
        ang = pos[None, :] * inv[:, None]
        return np.cos(ang).astype(np.float32), np.sin(ang).astype(np.float32)
    c128, s128 = tab(128)
    c64, s64 = tab(64)
    rope = np.zeros((128, 4, T), np.float32)
    rope[:, 0] = np.concatenate([c128, c128], 0)
    rope[:, 1] = np.concatenate([s128, s128], 0)
    rope[:, 2] = np.concatenate([c64, c64, c64, c64], 0)
    rope[:, 3] = np.concatenate([s64, s64, s64, s64], 0)
    negt = np.zeros((128, 8), np.float32)
    for jp in range(4):
        negt[:, jp] = 0.0 if jp < j else NEG
    for jp in range(3):
        negt[:, 4 + jp] = 0.0 if jp == j - 1 else NEG
    gneg = np.zeros((128, 8, 16), np.float32)
    for t in range(8):
        own = 4 * j + t // 2
        gneg[:, t, own:] = -1e30
    return {"rope": rope, "negt": negt, "gneg": gneg}


def gains_tensor(inp):
    g = np.zeros((128, NGC), np.float32)
    for l in range(2):
        g[:, gcol(l, "ffn1"):gcol(l, "ffn1") + 16] = _col(np.asarray(inp["ffn1_norm"][l]))
        g[:, gcol(l, "attn"):gcol(l, "attn") + 16] = _col(np.asarray(inp["attn_norm"][l]))
        g[:, gcol(l, "ffn2"):gcol(l, "ffn2") + 16] = _col(np.asarray(inp["ffn2_norm"][l]))
        g[:, gcol(l, "qn"):gcol(l, "qn") + 4] = _col(np.asarray(inp["mla_q_norm"][l]))
        g[:, gcol(l, "kvn"):gcol(l, "kvn") + 2] = _col(np.asarray(inp["mla_kv_norm"][l]))
    g[:, GFINAL:GFINAL + 16] = _col(np.asarray(inp["final_norm"]))
    return g


def common_maps(inp):
    sc = static_consts()
    gains = gains_tensor(inp)
    sinks = np.ascontiguousarray(np.broadcast_to(np.asarray(inp["swa_sinks"]).reshape(1, 16), (128, 16))).astype(np.float32)
    maps = []
    for r in range(8):
        m = dict(sc)
        m.update(core_consts(r % 4))
        m["gains"] = gains
        m["sinks"] = sinks
        maps.append(m)
    return maps


FUSED = False
SEQ = True


def _launch(stage, maps):
    nc = bass.Bass("TRN2", target_bir_lowering=False)
    Builder(nc, stage).build()
    res = run_bass_kernel_spmd(nc, maps, core_ids=list(range(8)))
    return res.results


def _wmaps(inp, m, ffn1_layers, attn_layers):
    for l in ffn1_layers:
        m["wg1_%d" % l] = inp["ffn1_w_gate"][l]
        m["wu1_%d" % l] = inp["ffn1_w_up"][l]
        m["wd1_%d" % l] = inp["ffn1_w_down"][l]
        m["win_%d" % l] = inp["w_in"][l]
        m["wuq_%d" % l] = inp["mla_w_uq"][l]
        m["wukv_%d" % l] = inp["mla_w_ukv"][l]
    for l in attn_layers:
        m["wout_%d" % l] = inp["w_out"][l]
        m["wg2_%d" % l] = inp["ffn2_w_gate"][l]
        m["wu2_%d" % l] = inp["ffn2_w_up"][l]
        m["wd2_%d" % l] = inp["ffn2_w_down"][l]


def kernel(**inputs):
    inp = {k: np.asarray(v) for k, v in inputs.items()}
    cm = common_maps(inp)
    x = np.ascontiguousarray(inp["x"].reshape(8, T, D))
    if SEQ:
        maps = []
        ccs = [core_consts(j) for j in range(4)]
        for r in range(8):
            m = dict(cm[r])
            m["rope"] = np.concatenate([c["rope"] for c in ccs], axis=1)
            m["negt"] = np.concatenate([c["negt"] for c in ccs], axis=1)
            m["gneg"] = np.concatenate([c["gneg"] for c in ccs], axis=1)
            m["x"] = np.ascontiguousarray(inp["x"][r // 4])
            m["negt_m"] = cm[r]["negt"]
            m["gneg_m"] = cm[r]["gneg"]
            row = (r % 4) * 128 + np.arange(128, dtype=np.uint32)
            idx = np.zeros((128, 8), np.uint32)
            idx[:, 0] = row
            for c4 in range(4):
                idx[:, 1 + c4] = row * 4 + c4
            m["myidx"] = idx
            _wmaps(inp, m, [0, 1], [0, 1])
            maps.append(m)
        res = _launch("SEQ", maps)
        return np.stack([res[r]["y"] for r in range(8)]).reshape(2, 4096, D)
    if FUSED:
        maps = []
        for r in range(8):
            m = dict(cm[r])
            m["x"] = x[r]
            _wmaps(inp, m, [0, 1], [0, 1])
            maps.append(m)
        res = _launch("ALL", maps)
        return np.stack([res[r]["y"] for r in range(8)]).reshape(2, 4096, D)

    def handoff(res, maps):
        for r in range(8):
            b = r // 4
            maps[r]["xs_i"] = res[r]["xs_o"]
            maps[r]["qs_i"] = res[r]["qs_o"]
            maps[r]["kvown_i"] = res[r]["kvown_o"]
            maps[r]["kvall_i"] = np.concatenate([res[4 * b + jj]["kvown_o"] for jj in range(4)], axis=0)

    maps = []
    for r in range(8):
        m = dict(cm[r])
        m["x"] = x[r]
        _wmaps(inp, m, [0], [])
        maps.append(m)
    res = _launch("A0", maps)
    maps = []
    for r in range(8):
        m = dict(cm[r])
        _wmaps(inp, m, [1], [0])
        maps.append(m)
    handoff(res, maps)
    res = _launch("B0A1", maps)
    maps = []
    for r in range(8):
        m = dict(cm[r])
        _wmaps(inp, m, [], [1])
        maps.append(m)
    handoff(res, maps)
    res = _launch("B1F", maps)
    return np.stack([res[r]["y"] for r in range(8)]).reshape(2, 4096, D)
```
